# Optimizing a Trainium2 kernel written in Bass

```python
import jax, jax.numpy as jnp
from jax import lax
import numpy as np

D_MODEL = 2048
BATCH = 1
SEQ = 8192
DEPTH = 1

CHUNK = 64
N_MEM = 256
LN_EPS = 1e-5
ALPHA = (2 * DEPTH) ** 0.25
BETA = (8 * DEPTH) ** -0.25

D_LRU = D_MODEL
LRU_HEADS = 16
LRU_HEAD_DIM = D_LRU // LRU_HEADS
CONV_WIDTH = 4
LRU_C = 8.0

D_RWKV = D_MODEL
RWKV_HEAD_DIM = 64
RWKV_HEADS = D_RWKV // RWKV_HEAD_DIM
D_DECAY_LORA = 96
D_AAA_LORA = 96
D_GATE_LORA = 256
GN_EPS = 64e-5
D_RWKV_IN = 3 * D_RWKV + D_DECAY_LORA + D_AAA_LORA + D_GATE_LORA

N_IN = 2 * D_LRU + D_RWKV_IN + 2 * D_MODEL

XATTN_HEADS = 4
XATTN_HEAD_DIM = D_MODEL // XATTN_HEADS

D_FF = -(-(8 * D_MODEL) // (3 * 256)) * 256

kernel_name = 'hybrid_rglru_rwkv7_deepnorm_encoder'


def layer_norm(x, g, b, eps=LN_EPS):
    xf = x.astype(jnp.float32)
    mu = jnp.mean(xf, axis=-1, keepdims=True)
    var = jnp.mean(jnp.square(xf - mu), axis=-1, keepdims=True)
    y = (xf - mu) * lax.rsqrt(var + eps) * g.astype(jnp.float32) + b.astype(jnp.float32)
    return y.astype(x.dtype)


def causal_depthwise_conv(u, w, b):
    t = u.shape[1]
    up = jnp.pad(u, ((0, 0), (CONV_WIDTH - 1, 0), (0, 0)))
    out = b
    for j in range(CONV_WIDTH):
        out = out + up[:, j:j + t] * w[j]
    return out


def chunked_linear_scan(a, b):
    bsz, t, c = a.shape
    nc = t // CHUNK
    a = a.reshape(bsz, nc, CHUNK, c)
    b = b.reshape(bsz, nc, CHUNK, c)

    def combine(lo, hi):
        return (lo[0] * hi[0], hi[0] * lo[1] + hi[1])

    a_cum, h_loc = lax.associative_scan(combine, (a, b), axis=2)
    _, h_end = lax.associative_scan(combine, (a_cum[:, :, -1], h_loc[:, :, -1]), axis=1)
    h_in = jnp.pad(h_end[:, :-1], ((0, 0), (1, 0), (0, 0)))
    return (h_loc + a_cum * h_in[:, :, None]).reshape(bsz, t, c)


def rg_lru(u, wa, ba, wx, bx, lam):
    bsz, t, _ = u.shape
    uh = u.reshape(bsz, t, LRU_HEADS, LRU_HEAD_DIM)
    r = jax.nn.sigmoid(jnp.einsum('btgi,gij->btgj', uh, wa) + ba).reshape(bsz, t, D_LRU)
    i = jax.nn.sigmoid(jnp.einsum('btgi,gij->btgj', uh, wx) + bx).reshape(bsz, t, D_LRU)
    log_a = -LRU_C * r.astype(jnp.float32) * jax.nn.softplus(-lam.astype(jnp.float32))
    a = jnp.exp(log_a)
    gated_x = jnp.sqrt(-jnp.expm1(2.0 * log_a)) * (i * u).astype(jnp.float32)
    return chunked_linear_scan(a, gated_x)


def rwkv7_time_mix(z, mu, w0, wB, a0, aB, gB, k_k, k_a, r_k, gn_g, gn_b):
    f32 = jnp.float32
    bsz, t, _ = z.shape
    z_prev = jnp.pad(z, ((0, 0), (1, 0), (0, 0)))[:, :-1]
    z = z + (z_prev - z) * mu
    r, k, v, w_lo, a_lo, g_lo = jnp.split(
        z, [D_RWKV, 2 * D_RWKV, 3 * D_RWKV, 3 * D_RWKV + D_DECAY_LORA,
            3 * D_RWKV + D_DECAY_LORA + D_AAA_LORA], axis=-1)
    w_log = -jax.nn.softplus(-(w0 + jnp.tanh(w_lo) @ wB).astype(f32)) - 0.5
    decay = jnp.exp(-jnp.exp(w_log))
    a = jax.nn.sigmoid((a0 + a_lo @ aB).astype(f32))
    g = jax.nn.sigmoid(g_lo) @ gB
    kf = k.astype(f32)
    kk = (kf * k_k.astype(f32)).reshape(bsz, t, RWKV_HEADS, RWKV_HEAD_DIM)
    kk = kk / jnp.maximum(jnp.linalg.norm(kk, axis=-1, keepdims=True), 1e-12)
    kf = kf * (1.0 + (a - 1.0) * k_a.astype(f32))

    def heads(y):
        return y.astype(f32).reshape(bsz, t, RWKV_HEADS, RWKV_HEAD_DIM)

    r_h, k_h, v_h, w_h, a_h = heads(r), heads(kf), heads(v), heads(decay), heads(a)
    b_h = kk * a_h

    def step(S, inp):
        r_t, w_t, k_t, v_t, kk_t, b_t = inp
        sa = jnp.einsum('bhvk,bhk->bhv', S, kk_t)
        S = S * w_t[:, :, None, :] - sa[..., None] * b_t[:, :, None, :] + v_t[..., None] * k_t[:, :, None, :]
        return S, jnp.einsum('bhvk,bhk->bhv', S, r_t)

    S0 = jnp.zeros((bsz, RWKV_HEADS, RWKV_HEAD_DIM, RWKV_HEAD_DIM), f32)
    xs = tuple(jnp.moveaxis(y, 1, 0) for y in (r_h, w_h, k_h, v_h, kk, b_h))
    _, o = lax.scan(step, S0, xs)
    o = jnp.moveaxis(o, 0, 1)
    m = jnp.mean(o, axis=-1, keepdims=True)
    var = jnp.mean(jnp.square(o - m), axis=-1, keepdims=True)
    o = (o - m) * lax.rsqrt(var + GN_EPS) * gn_g.astype(f32).reshape(RWKV_HEADS, RWKV_HEAD_DIM) \
        + gn_b.astype(f32).reshape(RWKV_HEADS, RWKV_HEAD_DIM)
    o = o + jnp.sum(r_h * k_h * r_k.astype(f32), axis=-1, keepdims=True) * v_h
    return (o.reshape(bsz, t, D_RWKV) * g.astype(f32)).astype(z.dtype)


def hybrid_mixer(h, w_in, conv_w, conv_b, lru_wa, lru_ba, lru_wx, lru_bx, lru_lambda,
                 rw_mu, rw_w0, rw_wB, rw_a0, rw_aB, rw_gB, rw_kk, rw_ka, rw_rk,
                 rw_gn_g, rw_gn_b, w_out):
    zin = h @ w_in
    u, gate_lru, z_rw, g_a, g_b = jnp.split(
        zin, [D_LRU, 2 * D_LRU, 2 * D_LRU + D_RWKV_IN, 2 * D_LRU + D_RWKV_IN + D_MODEL], axis=-1)
    u = causal_depthwise_conv(u, conv_w, conv_b)
    y_a = jax.nn.gelu(gate_lru) * rg_lru(u, lru_wa, lru_ba, lru_wx, lru_bx, lru_lambda).astype(h.dtype)
    y_b = rwkv7_time_mix(z_rw, rw_mu, rw_w0, rw_wB, rw_a0, rw_aB, rw_gB, rw_kk, rw_ka, rw_rk,
                         rw_gn_g, rw_gn_b)
    y = jax.nn.sigmoid(g_a) * y_a + jax.nn.sigmoid(g_b) * y_b
    return y @ w_out


def memory_cross_attention(h, mem, wq, wk, wv, wo):
    bsz, t, _ = h.shape
    q = (h @ wq).reshape(bsz, t, XATTN_HEADS, XATTN_HEAD_DIM)
    k = (mem @ wk).reshape(bsz, N_MEM, XATTN_HEADS, XATTN_HEAD_DIM)
    v = (mem @ wv).reshape(bsz, N_MEM, XATTN_HEADS, XATTN_HEAD_DIM)
    s = jnp.einsum('bthd,bmhd->bhtm', q, k).astype(jnp.float32) * (XATTN_HEAD_DIM ** -0.5)
    p = jax.nn.softmax(s, axis=-1).astype(h.dtype)
    o = jnp.einsum('bhtm,bmhd->bthd', p, v).reshape(bsz, t, D_MODEL)
    return o @ wo


def swiglu(h, wg, wu, wd):
    return (jax.nn.silu(h @ wg) * (h @ wu)) @ wd


def setup_inputs(seed: int = 0) -> dict:
    key = jax.random.key(seed)
    ks = iter(jax.random.split(key, 48))
    f32 = jnp.float32
    L = DEPTH

    def nrm(shape, scale):
        return scale * jax.random.normal(next(ks), shape, f32)

    def gain(shape):
        return 1.0 + nrm(shape, 0.02)

    u = jax.random.uniform(next(ks), (L, D_LRU), f32, 0.9, 0.999)
    a_init = u ** (1.0 / LRU_C)
    lru_lambda = jnp.log(a_init) - jnp.log1p(-a_init)
    n = jnp.arange(D_RWKV, dtype=f32) / (D_RWKV - 1)
    rw_w0 = -6.5 + 5.0 * n ** 0.85 + nrm((L, D_RWKV), 0.1)
    rw_mu = jax.random.uniform(next(ks), (L, D_RWKV_IN), f32)
    return {
        'x': nrm((BATCH, SEQ, D_MODEL), 1.0),
        'mem': nrm((BATCH, N_MEM, D_MODEL), 1.0),
        'ln_in_g': gain((D_MODEL,)),
        'ln_in_b': nrm((D_MODEL,), 0.02),
        'w_in': nrm((L, D_MODEL, N_IN), D_MODEL ** -0.5),
        'conv_w': nrm((L, CONV_WIDTH, D_LRU), CONV_WIDTH ** -0.5),
        'conv_b': nrm((L, D_LRU), 0.02),
        'lru_wa': nrm((L, LRU_HEADS, LRU_HEAD_DIM, LRU_HEAD_DIM), LRU_HEAD_DIM ** -0.5),
        'lru_ba': nrm((L, LRU_HEADS, LRU_HEAD_DIM), 0.02),
        'lru_wx': nrm((L, LRU_HEADS, LRU_HEAD_DIM, LRU_HEAD_DIM), LRU_HEAD_DIM ** -0.5),
        'lru_bx': nrm((L, LRU_HEADS, LRU_HEAD_DIM), 0.02),
        'lru_lambda': lru_lambda,
        'rw_mu': rw_mu,
        'rw_w0': rw_w0,
        'rw_wB': nrm((L, D_DECAY_LORA, D_RWKV), 0.5 * D_DECAY_LORA ** -0.5),
        'rw_a0': nrm((L, D_RWKV), 0.1),
        'rw_aB': nrm((L, D_AAA_LORA, D_RWKV), 0.5 * D_AAA_LORA ** -0.5),
        'rw_gB': nrm((L, D_GATE_LORA, D_RWKV), D_GATE_LORA ** -0.5),
        'rw_kk': 0.85 + nrm((L, D_RWKV), 0.02),
        'rw_ka': gain((L, D_RWKV)),
        'rw_rk': nrm((L, RWKV_HEADS, RWKV_HEAD_DIM), 0.1),
        'rw_gn_g': gain((L, D_RWKV)),
        'rw_gn_b': nrm((L, D_RWKV), 0.02),
        'w_out': nrm((L, D_MODEL, D_MODEL), BETA * D_MODEL ** -0.5),
        'ln1_g': gain((L, D_MODEL)),
        'ln1_b': nrm((L, D_MODEL), 0.02),
        'xa_wq': nrm((L, D_MODEL, D_MODEL), D_MODEL ** -0.5),
        'xa_wk': nrm((L, D_MODEL, D_MODEL), D_MODEL ** -0.5),
        'xa_wv': nrm((L, D_MODEL, D_MODEL), D_MODEL ** -0.5),
        'xa_wo': nrm((L, D_MODEL, D_MODEL), BETA * D_MODEL ** -0.5),
        'ln2_g': gain((L, D_MODEL)),
        'ln2_b': nrm((L, D_MODEL), 0.02),
        'ffn_wg': nrm((L, D_MODEL, D_FF), D_MODEL ** -0.5),
        'ffn_wu': nrm((L, D_MODEL, D_FF), D_MODEL ** -0.5),
        'ffn_wd': nrm((L, D_FF, D_MODEL), BETA * D_FF ** -0.5),
        'ln3_g': gain((L, D_MODEL)),
        'ln3_b': nrm((L, D_MODEL), 0.02),
    }


def reference(x, mem, ln_in_g, ln_in_b, w_in, conv_w, conv_b, lru_wa, lru_ba, lru_wx, lru_bx,
              lru_lambda, rw_mu, rw_w0, rw_wB, rw_a0, rw_aB, rw_gB, rw_kk, rw_ka, rw_rk,
              rw_gn_g, rw_gn_b, w_out, ln1_g, ln1_b, xa_wq, xa_wk, xa_wv, xa_wo, ln2_g, ln2_b,
              ffn_wg, ffn_wu, ffn_wd, ln3_g, ln3_b):
    h = layer_norm(x, ln_in_g, ln_in_b)
    for l in range(DEPTH):
        mix = hybrid_mixer(h, w_in[l], conv_w[l], conv_b[l], lru_wa[l], lru_ba[l], lru_wx[l],
                           lru_bx[l], lru_lambda[l], rw_mu[l], rw_w0[l], rw_wB[l], rw_a0[l],
                           rw_aB[l], rw_gB[l], rw_kk[l], rw_ka[l], rw_rk[l], rw_gn_g[l],
                           rw_gn_b[l], w_out[l])
        h = layer_norm(ALPHA * h + mix, ln1_g[l], ln1_b[l])
        xa = memory_cross_attention(h, mem, xa_wq[l], xa_wk[l], xa_wv[l], xa_wo[l])
        h = layer_norm(ALPHA * h + xa, ln2_g[l], ln2_b[l])
        ff = swiglu(h, ffn_wg[l], ffn_wu[l], ffn_wd[l])
        h = layer_norm(ALPHA * h + ff, ln3_g[l], ln3_b[l])
    return h
```

```python
import numpy as np
import ml_dtypes
from contextlib import ExitStack
import concourse.bass as bass
import concourse.mybir as mybir
from concourse.bass_utils import run_bass_kernel_spmd

F32 = mybir.dt.float32
BF16 = mybir.dt.bfloat16
AF = mybir.ActivationFunctionType
ALU = mybir.AluOpType

D = 2048
NIN = 14784
DFF = 5632
BLK = 512
ALPHA = 2 ** 0.25
LN_EPS = 1e-5
GN_EPS = 64e-5
WSC = 0.6065306597126334


class Buf:
    def __init__(self, t, name=""):
        self.t = t
        self.name = name
        self.w = {}
        self.r = {}

    def __getitem__(self, idx):
        return self.t[idx]


class Sched:
    def __init__(self, nc, es):
        self.nc = nc
        self.es = es
        self.engs = {"pe": nc.tensor, "act": nc.scalar, "dve": nc.vector, "pool": nc.gpsimd, "sp": nc.sync}
        self.ops = {k: [] for k in self.engs}
        self.cnt = {k: 0 for k in self.engs}
        self.sems = {}
        for k in ("pe", "act", "dve", "pool"):
            self.sems[k] = es.enter_context(nc.semaphore("sem_" + k))
        self.ndsem = 8
        self.dq = {}
        for q in ("sp", "act", "pool"):
            for i in range(self.ndsem):
                self.sems[("d", q, i)] = es.enter_context(nc.semaphore("dsem_%s_%d" % (q, i)))
            self.dq[q] = {"n": 0, "val": [0] * self.ndsem}
        self.final = []
        self.bar = {}

    def barrier(self):
        allw = {k: self.cnt[k] for k in ("pe", "act", "dve", "pool") if self.cnt[k] > 0}
        for q, st in self.dq.items():
            for i, v in enumerate(st["val"]):
                if v > 0:
                    allw[("d", q, i)] = v
        self.bar = {e: dict(allw) for e in self.ops}

    def _bar(self, eng, waits):
        for k, v in self.bar.pop(eng, {}).items():
            if waits.get(k, 0) < v:
                waits[k] = v

    def _deps(self, reads, writes):
        waits = {}

        def add(tok):
            if tok is None:
                return
            k, v = tok
            if waits.get(k, 0) < v:
                waits[k] = v
        for b in reads:
            for k, v in b.w.items():
                add((k, v))
        for b in writes:
            for k, v in b.w.items():
                add((k, v))
            for k, v in b.r.items():
                add((k, v))
        return waits

    def _mark(self, tok, reads, writes):
        k, v = tok
        for b in reads:
            if b.r.get(k, 0) < v:
                b.r[k] = v
        for b in writes:
            if b.w.get(k, 0) < v:
                b.w[k] = v

    def op(self, eng, method, reads, writes, *args, **kw):
        waits = self._deps(reads, writes)
        self._bar(eng, waits)
        if eng == "pe":
            waits.pop("pe", None)
        self.cnt[eng] += 1
        tok = (eng, self.cnt[eng])
        self._mark(tok, reads, writes)
        self.ops[eng].append((waits, method, args, kw, eng, 1))
        return tok

    def dma(self, q, out, in_, reads, writes, **kw):
        waits = self._deps(reads, writes)
        self._bar(q, waits)
        st = self.dq[q]
        i = st["n"] % self.ndsem
        st["n"] += 1
        key = ("d", q, i)
        if st["val"][i] > 0:
            if waits.get(key, 0) < st["val"][i]:
                waits[key] = st["val"][i]
        st["val"][i] += 16
        tok = (key, st["val"][i])
        self._mark(tok, reads, writes)
        kw = dict(kw)
        kw["out"] = out
        kw["in_"] = in_
        self.ops[q].append((waits, "dma_start", (), kw, key, 16))
        return tok

    def emit(self, block):
        nc = self.nc
        final = {}
        for k, v in self.final:
            if final.get(k, 0) < v:
                final[k] = v

        def mk(name):
            def body(e):
                seen = {}
                for waits, method, args, kw, semkey, inc in self.ops[name]:
                    for k, v in waits.items():
                        if seen.get(k, 0) < v:
                            e.wait_ge(self.sems[k], v)
                            seen[k] = v
                    ins = getattr(e, method)(*args, **kw)
                    ins.then_inc(self.sems[semkey], inc)
                if name == "sp":
                    for k, v in final.items():
                        e.wait_ge(self.sems[k], v)
            return body
        block.sync(mk("sp"))
        block.tensor(mk("pe"))
        block.scalar(mk("act"))
        block.vector(mk("dve"))
        block.gpsimd(mk("pool"))


class CMap:
    def __init__(self):
        self.off = {}
        self.n = 0

    def add(self, name, ncol):
        self.off[name] = self.n
        self.n += ncol


def make_cmap():
    cm = CMap()
    for nm, n in [("lng", 16), ("lnb", 16), ("mu_rw", 48), ("mu_lo", 4), ("w0", 16), ("a0", 16), ("kk", 16),
                  ("ka", 16), ("rk", 16), ("cw", 64), ("cb", 16), ("ba", 16), ("bx", 16), ("lam", 16)]:
        cm.add(nm, n)
    return cm


CM = make_cmap()


def build(nsub=16, nfull=2, dbg=None):
    nc = bass.Bass("TRN2", target_bir_lowering=False)
    es = ExitStack()
    T = nsub * BLK
    NT = BLK
    NCH = NT // 128

    def din(name, shape, dt=F32):
        return nc.dram_tensor(name, list(shape), dt, kind="ExternalInput").ap()

    small = dbg is not None and dbg[0] != "full"
    x_d = din("x", [T, D])
    win_d = din("w_in", [D, NIN])
    wout_d = din("w_out", [128, 128] if small else [D, D])
    wq_d = din("xa_wq", [128, 128] if small else [D, D])
    wk_d = din("xa_wk", [128, 128] if small else [D, D])
    wv_d = din("xa_wv", [128, 128] if small else [D, D])
    wo_d = din("xa_wo", [128, 128] if small else [D, D])
    wg_d = din("ffn_wg", [128, 128] if small else [D, DFF])
    wu_d = din("ffn_wu", [128, 128] if small else [D, DFF])
    wd_d = din("ffn_wd", [128, 128] if small else [DFF, D])
    mem_d = din("mem", [256, D])
    cvec_d = din("cvec", [128, CM.n])
    lwa_d = din("lru_wa", [128, 16, 128])
    lwx_d = din("lru_wx", [128, 16, 128])
    wB_d = din("rw_wB", [96, D])
    aB_d = din("rw_aB", [96, D])
    gB_d = din("rw_gB", [256, D])
    bc_d = din("bcast", [10, 128, D])
    bmask_d = din("bmask", [128, 16])
    y_d = nc.dram_tensor("y", [nfull * NT, D], F32, kind="ExternalOutput").ap()
    dbg_d = None
    if dbg is not None:
        dbg_d = nc.dram_tensor("dbg", list(dbg[1]), F32, kind="ExternalOutput").ap()

    wbf_in = Buf(win_d, "w_in")
    wbf_out = Buf(wout_d, "w_out")
    wbf_q = Buf(wq_d, "wq")
    wbf_k = Buf(wk_d, "wk")
    wbf_v = Buf(wv_d, "wv")
    wbf_o = Buf(wo_d, "wo")
    wbf_g = Buf(wg_d, "wg")
    wbf_u = Buf(wu_d, "wu")
    wbf_d = Buf(wd_d, "wd")

    S = Sched(nc, es)
    ms = ExitStack()
    scope = [es]

    def sb(name, shape, dt=F32):
        return Buf(scope[0].enter_context(nc.sbuf_tensor("s_" + name, list(shape), dt)), name)

    def ps(name, shape, dt=F32):
        return Buf(scope[0].enter_context(nc.psum_tensor("p_" + name, list(shape), dt)), name)

    cv = sb("cv", [128, CM.n])
    bmask = sb("bmask", [128, 16])
    ident_b = sb("ident_b", [128, 128], BF16)
    ident_f = sb("ident_f", [128, 128])
    ones_f = sb("ones_f", [128, 128])
    yT = sb("yT", [128, nfull, 16, NT], BF16)
    xt = sb("xt", [128, D])
    xn = sb("xn", [128, D], BF16)
    st6 = sb("st6", [128, 4, 6])
    mv = sb("mv", [128, 2])
    rstd = sb("rstd", [128, 1])
    wt = [sb("wt%d" % i, [128, 16, 512], BF16) for i in range(2)]
    pT0 = ps("pT0", [128, 8, 128], BF16)
    pT = [pT0, pT0]
    pA = [ps("pA%d" % i, [128, 512]) for i in range(1)]
    pM = [ps("pM%d" % i, [128, 512]) for i in range(6)]

    def C(name, j=0, n=1, rows=128):
        o = CM.off[name] + j
        return cv[0:rows, o:o + n]

    S.dma("sp", cv[:], cvec_d, [], [cv])
    S.dma("sp", bmask[:], bmask_d, [], [bmask])
    S.op("dve", "memset", [], [ones_f], ones_f[:], 1.0)
    S.op("pool", "memset", [], [ident_f], ident_f[:], 0.0)
    S.op("pool", "affine_select", [ident_f], [ident_f], out=ident_f[:], in_=ident_f[:], pattern=[[-1, 128]],
         compare_op=ALU.not_equal, fill=1.0, base=0, channel_multiplier=1)
    S.op("dve", "tensor_copy", [ident_f], [ident_b], out=ident_b[:], in_=ident_f[:])

    wt_i = [0]

    def load_w(wbuf, c0, ncol, k0=0, nk=16, q="pool"):
        t = wt[wt_i[0] % 2]
        wt_i[0] += 1
        for ka in range(0, nk, 4):
            kb = min(nk, ka + 4)
            S.dma(q, t[:, ka:kb, 0:ncol],
                  wbuf.t[(k0 + ka) * 128:(k0 + kb) * 128, c0:c0 + ncol].rearrange("(k p) c -> p k c", p=128), [], [t])
        return t

    pa_i = [0]

    def next_pA():
        p = pA[pa_i[0] % len(pA)]
        pa_i[0] += 1
        return p

    def ln_stats(src):
        for c4 in range(4):
            S.op("dve", "bn_stats", [src], [st6], out=st6[:, c4, :], in_=src[:, c4 * 512:(c4 + 1) * 512])
        S.op("dve", "bn_aggr", [st6], [mv], out=mv[:], in_=st6[:].rearrange("p a b -> p (a b)"))
        S.op("act", "activation", [mv], [rstd], out=rstd[:], in_=mv[:, 1:2], func=AF.Sqrt, bias=LN_EPS, scale=1.0)
        S.op("dve", "reciprocal", [rstd], [rstd], out=rstd[:], in_=rstd[:])

    def transpose16(src_bf, dstT, tok0, evac):
        for half in range(2):
            pt = pT[half]
            for k8 in range(8):
                kt = half * 8 + k8
                S.op("pe", "transpose", [src_bf, ident_b], [pt], out=pt[:, k8, :], in_=src_bf[:, kt * 128:(kt + 1) * 128],
                     identity=ident_b[:])
            for k8 in range(8):
                evac(pt, k8, half * 8 + k8)

    scope[0] = ms
    gm = sb("gm", [128, 16, 16])
    bm = sb("bm", [128, 16, 16])
    omk = sb("omk", [128, 16])
    cneg = sb("cneg", [128, 16])
    cneg2 = sb("cneg2", [128, 16])
    mSU4 = sb("mSU4", [128, 512], BF16)
    mIU4 = sb("mIU4", [128, 512], BF16)
    mSL4 = sb("mSL4", [128, 512], BF16)
    id4 = sb("id4", [128, 512], BF16)
    bones = sb("bones", [128, 128], BF16)
    hsel = sb("hsel", [128, 2], BF16)
    hm = sb("hm", [128, 2])
    wa_bf = sb("wa_bf", [128, 4, 128], BF16)
    wx_bf = sb("wx_bf", [128, 4, 128], BF16)
    wB_bf = sb("wB_bf", [96, 128], BF16)
    aB_bf = sb("aB_bf", [96, 128], BF16)
    gB_bf = sb("gB_bf", [128, 2, 256], BF16)
    carry = sb("carry", [128, 64])
    ucarry = sb("ucarry", [128, 16, 3])
    hcarry = sb("hcarry", [128, 16])
    Sst = [sb("Sst%d" % i, [128, 128]) for i in range(16)]
    Sbf = [sb("Sbf%d" % i, [128, 128], BF16) for i in range(16)]
    hT = sb("hT", [128, 16, NT], BF16)


    tmpm = xt
    for (msk, pat, cmp_, cm_) in [(mSU4, [[1, 128]], ALU.is_gt, -1), (mIU4, [[1, 128]], ALU.is_ge, -1),
                                  (mSL4, [[-1, 128]], ALU.is_gt, 1)]:
        S.op("pool", "affine_select", [ones_f], [tmpm], out=tmpm[:, 0:128], in_=ones_f[:], pattern=pat,
             compare_op=cmp_, fill=0.0, base=0, channel_multiplier=cm_)
        for h in range(4):
            S.op("dve", "tensor_copy", [tmpm], [msk], out=msk[:, h * 128:(h + 1) * 128], in_=tmpm[:, 0:128])
    for h in range(4):
        S.op("dve", "tensor_copy", [ident_f], [id4], out=id4[:, h * 128:(h + 1) * 128], in_=ident_f[:])
    S.op("pool", "memset", [], [bones], bones[:], 0.0)
    S.op("pool", "memset", [bones], [bones], bones[0:64, 0:64], 1.0)
    S.op("pool", "memset", [bones], [bones], bones[64:128, 64:128], 1.0)
    S.op("pool", "memset", [], [hsel], hsel[:], 0.0)
    S.op("pool", "memset", [hsel], [hsel], hsel[0:64, 0:1], 1.0)
    S.op("pool", "memset", [hsel], [hsel], hsel[64:128, 1:2], 1.0)
    S.op("pool", "memset", [], [hm], hm[:], 0.0)
    S.op("pool", "memset", [hm], [hm], hm[0:64, 0:1], 1.0)
    S.op("pool", "memset", [hm], [hm], hm[64:128, 1:2], 1.0)
    S.op("pool", "memset", [], [carry], carry[:], 0.0)
    S.op("pool", "memset", [], [ucarry], ucarry[:], 0.0)
    S.op("pool", "memset", [], [hcarry], hcarry[:], 0.0)
    for i in range(16):
        S.op("pool", "memset", [], [Sst[i]], Sst[i][:], 0.0)
        S.op("pool", "memset", [], [Sbf[i]], Sbf[i][:], 0.0)
    for b in range(nsub):
        S.op("dve", "tensor_scalar", [cv, bmask], [gm], out=gm[:, b, :], in0=C("lng", 0, 16), scalar1=bmask[:, b:b + 1],
             scalar2=None, op0=ALU.mult)
        S.op("dve", "tensor_scalar", [cv, bmask], [bm], out=bm[:, b, :], in0=C("lnb", 0, 16), scalar1=bmask[:, b:b + 1],
             scalar2=None, op0=ALU.mult)
    S.op("dve", "tensor_scalar", [cv], [omk], out=omk[:], in0=C("ka", 0, 16), scalar1=-1.0, scalar2=1.0,
         op0=ALU.mult, op1=ALU.add)
    S.op("act", "activation", [cv], [cneg], out=cneg[:], in_=C("lam", 0, 16), func=AF.Exp, scale=-1.0)
    S.op("act", "activation", [cneg], [cneg], out=cneg[:], in_=cneg[:], func=AF.Ln, bias=1.0, scale=1.0)
    S.op("dve", "tensor_scalar", [cneg], [cneg2], out=cneg2[:], in0=cneg[:], scalar1=-16.0, scalar2=None, op0=ALU.mult)
    S.op("dve", "tensor_scalar", [cneg], [cneg], out=cneg[:], in0=cneg[:], scalar1=-8.0, scalar2=None, op0=ALU.mult)

    NTMP = 10
    tmps = [sb("tmp%d" % i, [128, NT]) for i in range(NTMP)]
    zs = [sb("zs%d" % i, [128, NT + 3]) for i in range(1)]
    zs_i = [0]
    twlo = sb("twlo", [96, NT], BF16)
    alo = sb("alo", [96, NT], BF16)
    sg = sb("sg", [128, 2, NT], BF16)
    ktl = [sb("ktl%d" % q, [128, NT], BF16) for q in range(2)]
    btl = [sb("btl%d" % q, [128, NT], BF16) for q in range(2)]
    atl = [sb("atl%d" % q, [128, NT], BF16) for q in range(4)]
    rtl = [sb("rtl%d" % q, [128, NT], BF16) for q in range(4)]
    vbf = [sb("vbf%d" % q, [128, NT], BF16) for q in range(2)]
    rkk = [sb("rkk%d" % q, [128, NT], BF16) for q in range(2)]
    sqb = sb("sqb", [128, NT], BF16)
    ucb = sqb
    gam = [sb("gam%d" % q, [128, NCH]) for q in range(2)]
    KTs = sb("KTs", [128, NCH, 2, 128], BF16)
    BTs = sb("BTs", [128, NCH, 2, 128], BF16)
    VTs = sb("VTs", [128, NCH, 2, 128], BF16)
    PT2_s = [[sb("PT2%d_%d" % (st, i), [128, 4, 2, 128], BF16) for i in range(2)] for st in range(2)]
    PTb_s = [[sb("PTb%d_%d" % (st, i), [128, 512], BF16) for i in range(2)] for st in range(2)]
    Tb_s = [[sb("Tb%d_%d" % (st, i), [128, 512], BF16) for i in range(1)] for st in range(2)]
    Mka_s = [sb("Mka%d" % st, [128, 512], BF16) for st in range(2)]
    Mkr_s = [sb("Mkr%d" % st, [128, 512], BF16) for st in range(2)]
    Mbr_s = [sb("Mbr%d" % st, [128, 512], BF16) for st in range(2)]
    W1s = sb("W1s", [128, 256], BF16)
    UTs = sb("UTs", [128, 256], BF16)
    Stmp = tmps[0]
    gnG = sb("gnG", [128, 256])
    gnB = sb("gnB", [128, 256])
    gst = sb("gst", [128, 4, 6])
    gmv = sb("gmv", [128, 4, 2])
    grs = sb("grs", [128, 4])
    bsum = sb("bsum", [128, 4])
    onb = tmps[1]
    ofb = tmps[2]
    ybb = W1s

    wsm = [sb("wsm%d" % i, [128, 8, 128], BF16) for i in range(1)]

    def proj_small(c0):
        t = wsm[0]
        p = next_pA()
        for hf in range(2):
            for ka in range(0, 8, 4):
                k0 = hf * 8 + ka
                S.dma("pool", t[:, ka:ka + 4, :], wbf_in.t[k0 * 128:(k0 + 4) * 128, c0:c0 + 128].rearrange("(k p) c -> p k c", p=128),
                      [], [t])
            for k8 in range(8):
                kt = hf * 8 + k8
                S.op("pe", "matmul", [t, hT], [p], out=p[:, :], lhsT=t[:, k8, :], rhs=hT[:, kt, :], start=(kt == 0), stop=(kt == 15))
        return p

    def shifted(p, M, cid, mucol, out_ap, out_buf, post=None):
        z = zs[0]
        zs_i[0] += 1
        d = tmps[NTMP - 1]
        S.op("pool", "tensor_copy", [carry], [z], out=z[0:M, 0:1], in_=carry[0:M, cid:cid + 1])
        S.op("act", "activation", [p], [z], out=z[0:M, 1:NT + 1], in_=p[0:M, 0:NT], func=AF.Copy)
        S.op("pool", "tensor_copy", [z], [carry], out=carry[0:M, cid:cid + 1], in_=z[0:M, NT:NT + 1])
        S.op("pool", "tensor_tensor", [z], [d], out=d[0:M, :], in0=z[0:M, 0:NT], in1=z[0:M, 1:NT + 1], op=ALU.subtract)
        S.op("dve", "scalar_tensor_tensor", [d, z, cv], [out_buf], out=out_ap, in0=d[0:M, :], scalar=mucol,
             in1=z[0:M, 1:NT + 1], op0=ALU.mult, op1=ALU.add)

    def proj_tile(wtile, cofs, M):
        p = next_pA()
        for kt in range(16):
            S.op("pe", "matmul", [wtile, hT], [p], out=p[0:M, :], lhsT=wtile[:, kt, cofs:cofs + M],
                 rhs=hT[:, kt, :], start=(kt == 0), stop=(kt == 15))
        return p

    def dump(buf, ap, rows):
        if dbg_d is not None:
            S.final.append(S.dma("sp", dbg_d[0:rows, :], ap, [buf], []))

    for b in range(nsub):
        full = (b >= nsub - nfull)
        fi = b - (nsub - nfull)
        for tt in range(NCH):
            r0 = b * NT + tt * 128
            S.dma("sp", xt[:], x_d[r0:r0 + 128, :], [], [xt])
            ln_stats(xt)
            S.op("dve", "tensor_scalar", [xt, mv, rstd], [xn], out=xn[:], in0=xt[:], scalar1=mv[:, 0:1],
                 scalar2=rstd[:, 0:1], op0=ALU.subtract, op1=ALU.mult)

            def ev(pt, k8, kt, tt=tt, b=b):
                if k8 % 2 == 0:
                    S.op("act", "activation", [pt, gm, bm], [hT], out=hT[:, kt, tt * 128:(tt + 1) * 128], in_=pt[:, k8, :],
                         func=AF.Identity, scale=gm[:, b, kt:kt + 1], bias=bm[:, b, kt:kt + 1])
                else:
                    S.op("dve", "tensor_scalar", [pt, gm, bm], [hT], out=hT[:, kt, tt * 128:(tt + 1) * 128],
                         in0=pt[:, k8, :], scalar1=gm[:, b, kt:kt + 1], scalar2=bm[:, b, kt:kt + 1],
                         op0=ALU.mult, op1=ALU.add)
            transpose16(xn, hT, tt * 128, ev)

        wl = load_w(wbf_in, 6144, 448)
        t0 = tmps[0]
        p = proj_tile(wl, 0, 96)
        shifted(p, 96, 48, C("mu_lo", 0, 1, 96), t0[0:96, :], t0)
        S.op("act", "activation", [t0], [twlo], out=twlo[:], in_=t0[0:96, :], func=AF.Tanh)
        p = proj_tile(wl, 96, 96)
        shifted(p, 96, 49, C("mu_lo", 1, 1, 96), alo[:], alo)
        precarry = (b == nsub - nfull - 1)
        if full:
            for j in range(2):
                p = proj_tile(wl, 192 + j * 128, 128)
                shifted(p, 128, 50 + j, C("mu_lo", 2 + j), t0[:], t0)
                S.op("act", "activation", [t0], [sg], out=sg[:, j, :], in_=t0[:], func=AF.Sigmoid)
        elif precarry:
            for j in range(2):
                p = proj_tile(wl, 192 + j * 128, 128)
                S.op("act", "activation", [p], [carry], out=carry[:, 50 + j:51 + j], in_=p[:, NT - 1:NT], func=AF.Copy)

        stg = dbg[0] if dbg is not None else None
        for hg in range(8):
            if stg in ("st1", "st4") or (stg in ("st2", "st3") and hg > 0):
                break
            wkv = load_w(wbf_in, hg * 768, 512)
            wr = load_w(wbf_in, hg * 768 + 512, 256) if (full or precarry) else None
            if precarry:
                for q in range(2):
                    p = proj_tile(wr, q * 128, 128)
                    S.op("act", "activation", [p], [carry], out=carry[:, hg * 6 + 4 + q:hg * 6 + 5 + q], in_=p[:, NT - 1:NT],
                         func=AF.Copy)
            if full:
                S.dma("pool", gB_bf[:], gB_d[:, hg * 256:(hg + 1) * 256].rearrange("(k p) c -> p k c", p=128), [], [gB_bf])
                S.dma("sp", gnG[:], bc_d[0, :, hg * 256:(hg + 1) * 256], [], [gnG])
                S.dma("sp", gnB[:], bc_d[1, :, hg * 256:(hg + 1) * 256], [], [gnB])
            for q in range(2):
                pt_ = hg * 2 + q
                kp, av, sv, cum, e1, e2, kkn, kf, rp = tmps[0:9]
                rn = e2
                bb = av
                p = proj_tile(wkv, q * 128, 128)
                shifted(p, 128, hg * 6 + q, C("mu_rw", hg * 6 + q), kp[:], kp)
                p = proj_tile(wkv, 256 + q * 128, 128)
                shifted(p, 128, hg * 6 + 2 + q, C("mu_rw", hg * 6 + 2 + q), vbf[q][:], vbf[q])
                if full:
                    p = proj_tile(wr, q * 128, 128)
                    shifted(p, 128, hg * 6 + 4 + q, C("mu_rw", hg * 6 + 4 + q), rp[:], rp)
                S.dma("pool", wB_bf[:], wB_d[:, pt_ * 128:(pt_ + 1) * 128], [], [wB_bf])
                S.dma("pool", aB_bf[:], aB_d[:, pt_ * 128:(pt_ + 1) * 128], [], [aB_bf])
                p = next_pA()
                S.op("pe", "matmul", [wB_bf, twlo], [p], out=p[:, :], lhsT=wB_bf[0:96, :],
                     rhs=twlo[0:96, :], start=True, stop=True)
                S.op("act", "activation", [p, cv], [sv], out=sv[:], in_=p[:, :], func=AF.Sigmoid, bias=C("w0", pt_), scale=1.0)
                p = next_pA()
                S.op("pe", "matmul", [aB_bf, alo], [p], out=p[:, :], lhsT=aB_bf[0:96, :],
                     rhs=alo[0:96, :], start=True, stop=True)
                S.op("act", "activation", [p, cv], [av], out=av[:], in_=p[:, :], func=AF.Sigmoid, bias=C("a0", pt_), scale=1.0)
                S.op("act", "activation", [kp, cv], [sqb], out=sqb[:], in_=kp[:], func=AF.Square, scale=C("kk", pt_))
                p = next_pA()
                S.op("pe", "matmul", [bones, sqb], [p], out=p[:, :], lhsT=bones[:], rhs=sqb[:], start=True, stop=True)
                S.op("act", "activation", [p], [rn], out=rn[:], in_=p[:, :], func=AF.Sqrt)
                S.op("dve", "tensor_scalar", [rn], [rn], out=rn[:], in0=rn[:], scalar1=1e-12, scalar2=None, op0=ALU.max)
                S.op("dve", "reciprocal", [rn], [rn], out=rn[:], in_=rn[:])
                S.op("dve", "scalar_tensor_tensor", [kp, cv, rn], [kkn], out=kkn[:], in0=kp[:], scalar=C("kk", pt_),
                     in1=rn[:], op0=ALU.mult, op1=ALU.mult)
                S.op("dve", "tensor_scalar", [av, cv, omk], [e1], out=e1[:], in0=av[:], scalar1=C("ka", pt_),
                     scalar2=omk[:, pt_:pt_ + 1], op0=ALU.mult, op1=ALU.add)
                S.op("dve", "tensor_tensor", [kp, e1], [kf], out=kf[:], in0=kp[:], in1=e1[:], op=ALU.mult)
                S.op("pool", "tensor_tensor", [kkn, av], [bb], out=bb[:], in0=kkn[:], in1=av[:], op=ALU.mult)
                if full:
                    S.op("dve", "scalar_tensor_tensor", [rp, cv, kf], [rkk[q]], out=rkk[q][:], in0=rp[:], scalar=C("rk", pt_),
                         in1=kf[:], op0=ALU.mult, op1=ALU.mult)
                for c in range(NCH):
                    S.op("dve", "tensor_tensor_scan", [sv, ones_f], [cum], out=cum[:, c * 128:(c + 1) * 128],
                         data0=ones_f[:, 0:128], data1=sv[:, c * 128:(c + 1) * 128], initial=0.0, op0=ALU.mult, op1=ALU.add)
                S.op("act", "activation", [cum], [gam[q]], out=gam[q][:],
                     in_=cum[:].rearrange("p (c t) -> p c t", t=128)[:, :, 127], func=AF.Exp, scale=-WSC)
                S.op("act", "activation", [cum], [e1], out=e1[:], in_=cum[:], func=AF.Exp, scale=WSC)
                S.op("dve", "tensor_tensor", [kf, e1], [ktl[q]], out=ktl[q][:], in0=kf[:], in1=e1[:], op=ALU.mult)
                S.op("dve", "tensor_tensor", [bb, e1], [btl[q]], out=btl[q][:], in0=bb[:], in1=e1[:], op=ALU.mult)
                if full:
                    S.op("act", "activation", [cum], [e2], out=e2[:], in_=cum[:], func=AF.Exp, scale=-WSC)
                    for hh in range(2):
                        S.op("dve", "scalar_tensor_tensor", [rp, hm, e2], [rtl[2 * q + hh]], out=rtl[2 * q + hh][:], in0=rp[:],
                             scalar=hm[:, hh:hh + 1], in1=e2[:], op0=ALU.mult, op1=ALU.mult)
                S.op("pool", "tensor_tensor", [cum, sv], [cum], out=cum[:], in0=cum[:], in1=sv[:], op=ALU.subtract)
                S.op("act", "activation", [cum], [e2], out=e2[:], in_=cum[:], func=AF.Exp, scale=-WSC)
                S.op("dve", "tensor_scalar", [e2], [e2], out=e2[:], in0=e2[:], scalar1=-1.0, scalar2=None, op0=ALU.mult)
                for hh in range(2):
                    S.op("dve", "scalar_tensor_tensor", [kkn, hm, e2], [atl[2 * q + hh]], out=atl[2 * q + hh][:], in0=kkn[:],
                         scalar=hm[:, hh:hh + 1], in1=e2[:], op0=ALU.mult, op1=ALU.mult)
                for (src, dst, pti) in [(ktl[q], KTs, 0), (btl[q], BTs, 1), (vbf[q], VTs, 0)]:
                    ptt = pT[pti]
                    for c in range(NCH):
                        S.op("pe", "transpose", [src, ident_b], [ptt], out=ptt[:, c, :], in_=src[:, c * 128:(c + 1) * 128],
                             identity=ident_b[:])
                    S.op("act", "activation", [ptt], [dst], out=dst[:, :, q, :], in_=ptt[:, 0:NCH, :], func=AF.Copy)

            def hd(h, hg=hg):
                q_, hh = h // 2, h % 2
                return q_, hg * 2 + q_, hh

            def t_phase(c, st, full=full):
                Cc = slice(c * 128, (c + 1) * 128)
                b0, b1, b2 = pM[3 * st], pM[3 * st + 1], pM[3 * st + 2]
                PT2, PTb, Tfin = PT2_s[st], PTb_s[st], Tb_s[st][0]
                for h in range(4):
                    q_, pt_, hh = hd(h)
                    S.op("pe", "matmul", [btl[q_], atl[h]], [b1], out=b1[:, h * 128:(h + 1) * 128],
                         lhsT=btl[q_][:, Cc], rhs=atl[h][:, Cc], start=True, stop=True)
                for h in range(4):
                    q_, pt_, hh = hd(h)
                    S.op("pe", "matmul", [btl[q_], atl[h]], [b2], out=b2[:, h * 128:(h + 1) * 128],
                         lhsT=atl[h][:, Cc], rhs=btl[q_][:, Cc], start=True, stop=True)
                for h in range(4):
                    q_, pt_, hh = hd(h)
                    S.op("pe", "matmul", [ktl[q_], atl[h]], [b0], out=b0[:, h * 128:(h + 1) * 128],
                         lhsT=ktl[q_][:, Cc], rhs=atl[h][:, Cc], start=True, stop=True)
                yield
                v4 = "p (h t) -> p h t"
                S.op("dve", "tensor_tensor", [b1, mSU4], [PT2[0]], out=PT2[0][:, :, 0, :], in0=b1[:, :].rearrange(v4, h=4),
                     in1=mSU4[:].rearrange(v4, h=4), op=ALU.mult)
                S.op("dve", "tensor_tensor", [b2, mSL4], [PTb[0]], out=PTb[0][:], in0=b2[:, :], in1=mSL4[:], op=ALU.mult)
                S.op("pool", "tensor_copy", [id4], [PT2[0]], out=PT2[0][:, :, 1, :], in_=id4[:].rearrange(v4, h=4))
                S.op("dve", "tensor_tensor", [b0, mSU4], [Mka_s[st]], out=Mka_s[st][:], in0=b0[:, :], in1=mSU4[:], op=ALU.mult)
                yield
                cur = 0
                v2 = "p (h s t) -> p h s t"
                for k in range(6):
                    nxt = 1 - cur
                    last = (k == 5)
                    for h in range(4):
                        bb_ = b0 if h < 2 else b1
                        hl = h % 2
                        S.op("pe", "matmul", [PTb[cur], PT2[cur]], [bb_], out=bb_[:, hl * 256:(hl + 1) * 256],
                             lhsT=PTb[cur][:, h * 128:(h + 1) * 128], rhs=PT2[cur][:, h, :, :], start=True, stop=True)
                    for h in range(4):
                        hs_ = slice(h * 128, (h + 1) * 128)
                        S.op("pe", "matmul", [PTb[cur], PT2[cur]], [b2], out=b2[:, hs_], lhsT=PT2[cur][:, h, 0, :],
                             rhs=PTb[cur][:, hs_], start=True, stop=True)
                    yield
                    S.op("act", "activation", [b2], [PTb[nxt]], out=PTb[nxt][:], in_=b2[:, :], func=AF.Copy)
                    for bi, bb_ in enumerate((b0, b1)):
                        bv = bb_[:, :].rearrange(v2, h=2, s=2)
                        hs2 = slice(2 * bi, 2 * bi + 2)
                        if not last:
                            S.op("dve", "tensor_copy", [bb_], [PT2[nxt]], out=PT2[nxt][:, hs2, 0, :], in_=bv[:, :, 0, :])
                        S.op("dve", "tensor_tensor", [bb_, PT2[cur]], [PT2[nxt]], out=PT2[nxt][:, hs2, 1, :], in0=bv[:, :, 1, :],
                             in1=PT2[cur][:, hs2, 1, :], op=ALU.add)
                    cur = nxt
                    yield
                for h in range(4):
                    hs_ = slice(h * 128, (h + 1) * 128)
                    S.op("pe", "matmul", [PTb[cur], PT2[cur]], [b0], out=b0[:, hs_], lhsT=PTb[cur][:, hs_],
                         rhs=PT2[cur][:, h, 1, :], start=True, stop=True)
                yield
                S.op("dve", "tensor_tensor", [b0, PT2[cur]], [Tfin], out=Tfin[:].rearrange(v4, h=4), in0=b0[:, :].rearrange(v4, h=4),
                     in1=PT2[cur][:, :, 1, :], op=ALU.add)
                yield
                tcur = 0
                assert tcur == 0
                if full:
                    for h in range(4):
                        q_, pt_, hh = hd(h)
                        S.op("pe", "matmul", [ktl[q_], rtl[h]], [b0], out=b0[:, h * 128:(h + 1) * 128],
                             lhsT=ktl[q_][:, Cc], rhs=rtl[h][:, Cc], start=True, stop=True)
                    S.op("dve", "tensor_tensor", [b0, mIU4], [Mkr_s[st]], out=Mkr_s[st][:], in0=b0[:, :], in1=mIU4[:], op=ALU.mult)
                    yield
                    for h in range(4):
                        q_, pt_, hh = hd(h)
                        S.op("pe", "matmul", [btl[q_], rtl[h]], [b1], out=b1[:, h * 128:(h + 1) * 128],
                             lhsT=btl[q_][:, Cc], rhs=rtl[h][:, Cc], start=True, stop=True)
                    S.op("dve", "tensor_tensor", [b1, mIU4], [Mbr_s[st]], out=Mbr_s[st][:], in0=b1[:, :], in1=mIU4[:], op=ALU.mult)
                    yield

            def state_phase(c, st, full=full, hg=hg):
                Cc = slice(c * 128, (c + 1) * 128)
                b0, b1, b2 = pM[3 * st], pM[3 * st + 1], pM[3 * st + 2]
                Tf = Tb_s[st][0]
                Mka, Mkr, Mbr = Mka_s[st], Mkr_s[st], Mbr_s[st]
                for h in range(4):
                    q_, pt_, hh = hd(h)
                    vs = slice(64 * hh, 64 * hh + 64)
                    S.op("pe", "matmul", [atl[h], Sbf[pt_]], [b0], out=b0[:, h * 64:(h + 1) * 64], lhsT=atl[h][:, Cc],
                         rhs=Sbf[pt_][:, vs], start=True, stop=False)
                    S.op("pe", "matmul", [Mka, VTs], [b0], out=b0[:, h * 64:(h + 1) * 64], lhsT=Mka[:, h * 128:(h + 1) * 128],
                         rhs=VTs[:, c, q_, vs], start=False, stop=True)
                S.op("act", "activation", [b0], [W1s], out=W1s[:], in_=b0[:, 0:256], func=AF.Copy)
                for h in range(4):
                    S.op("pe", "matmul", [Tf, W1s], [b1], out=b1[:, h * 64:(h + 1) * 64], lhsT=Tf[:, h * 128:(h + 1) * 128],
                         rhs=W1s[:, h * 64:(h + 1) * 64], start=True, stop=True)
                S.op("act", "activation", [b1], [UTs], out=UTs[:], in_=b1[:, 0:256], func=AF.Copy)
                if full:
                    for h in range(4):
                        q_, pt_, hh = hd(h)
                        vs = slice(64 * hh, 64 * hh + 64)
                        os_ = slice(h * 64, (h + 1) * 64)
                        S.op("pe", "matmul", [rtl[h], Sbf[pt_]], [b2], out=b2[:, os_], lhsT=rtl[h][:, Cc],
                             rhs=Sbf[pt_][:, vs], start=True, stop=False)
                        S.op("pe", "matmul", [Mkr, VTs], [b2], out=b2[:, os_], lhsT=Mkr[:, h * 128:(h + 1) * 128],
                             rhs=VTs[:, c, q_, vs], start=False, stop=False)
                        S.op("pe", "matmul", [Mbr, UTs], [b2], out=b2[:, os_], lhsT=Mbr[:, h * 128:(h + 1) * 128],
                             rhs=UTs[:, os_], start=False, stop=True)
                for q_ in range(2):
                    qs = slice(q_ * 128, (q_ + 1) * 128)
                    ss = slice(256 + q_ * 128, 384 + q_ * 128)
                    S.op("pe", "matmul", [KTs, VTs], [b1], out=b1[:, ss], lhsT=KTs[:, c, q_, :], rhs=VTs[:, c, q_, :],
                         start=True, stop=False)
                    S.op("pe", "matmul", [BTs, UTs], [b1], out=b1[:, ss], lhsT=BTs[:, c, q_, :], rhs=UTs[:, qs],
                         start=False, stop=True)
                for q_ in range(2):
                    pt_ = hg * 2 + q_
                    ss = slice(256 + q_ * 128, 384 + q_ * 128)
                    S.op("dve", "tensor_tensor", [b1, Sst[pt_]], [Stmp], out=Stmp[:, 0:128], in0=b1[:, ss], in1=Sst[pt_][:],
                         op=ALU.add)
                    S.op("act", "activation", [Stmp, gam[q_]], [Sst[pt_]], out=Sst[pt_][:], in_=Stmp[:, 0:128], func=AF.Identity,
                         scale=gam[q_][:, c:c + 1])
                    S.op("dve", "tensor_scalar", [Stmp, gam[q_]], [Sbf[pt_]], out=Sbf[pt_][:], in0=Stmp[:, 0:128],
                         scalar1=gam[q_][:, c:c + 1], scalar2=None, op0=ALU.mult)
                if full:
                    for h in range(4):
                        os_ = slice(h * 64, (h + 1) * 64)
                        S.op("dve", "bn_stats", [b2], [gst], out=gst[:, h, :], in_=b2[:, os_])
                        S.op("dve", "bn_aggr", [gst], [gmv], out=gmv[:, h, :], in_=gst[:, h, :])
                    S.op("act", "activation", [gmv], [grs], out=grs[:], in_=gmv[:, :, 1], func=AF.Sqrt, bias=GN_EPS, scale=1.0)
                    S.op("dve", "reciprocal", [grs], [grs], out=grs[:], in_=grs[:])
                    for h in range(4):
                        os_ = slice(h * 64, (h + 1) * 64)
                        S.op("dve", "tensor_scalar", [b2, gmv, grs], [onb], out=onb[:, os_], in0=b2[:, os_],
                             scalar1=gmv[:, h, 0:1], scalar2=grs[:, h:h + 1], op0=ALU.subtract, op1=ALU.mult)
                    S.op("dve", "tensor_tensor", [onb, gnG], [onb], out=onb[:, 0:256], in0=onb[:, 0:256], in1=gnG[:], op=ALU.mult)
                    S.op("pool", "tensor_tensor", [onb, gnB], [onb], out=onb[:, 0:256], in0=onb[:, 0:256], in1=gnB[:], op=ALU.add)
                    for q_ in range(2):
                        S.op("pe", "matmul", [rkk[q_], hsel], [b0], out=b0[:, 256 + 2 * q_:258 + 2 * q_], lhsT=rkk[q_][:, Cc],
                             rhs=hsel[:], start=True, stop=True)
                    S.op("act", "activation", [b0], [bsum], out=bsum[:], in_=b0[:, 256:260], func=AF.Copy)
                    for h in range(4):
                        q_, pt_, hh = hd(h)
                        os_ = slice(h * 64, (h + 1) * 64)
                        S.op("dve", "scalar_tensor_tensor", [VTs, bsum, onb], [ofb], out=ofb[:, os_],
                             in0=VTs[:, c, q_, 64 * hh:64 * hh + 64], scalar=bsum[:, h:h + 1], in1=onb[:, os_],
                             op0=ALU.mult, op1=ALU.add)
                    for j in range(2):
                        S.op("pe", "matmul", [sg, gB_bf], [b2], out=b2[:, 256:512], lhsT=sg[:, j, Cc],
                             rhs=gB_bf[:, j, :], start=(j == 0), stop=(j == 1))
                    S.op("dve", "tensor_tensor", [ofb, b2], [ybb], out=ybb[:], in0=ofb[:, 0:256], in1=b2[:, 256:512], op=ALU.mult)
                    for q_ in range(2):
                        S.op("pe", "transpose", [ybb, ident_b], [pT[1]], out=pT[1][:, 4 + q_, :], in_=ybb[:, q_ * 128:(q_ + 1) * 128],
                             identity=ident_b[:])
                    for q_ in range(2):
                        S.op("act", "activation", [pT[1]], [yT], out=yT[:, fi, hg * 2 + q_, Cc], in_=pT[1][:, 4 + q_, :], func=AF.Copy)

            if stg != "st2":
                for cp in range(0, NCH, 2):
                    gens = [t_phase(cp, 0), t_phase(cp + 1, 1)]
                    alive = [True, True]
                    while any(alive):
                        for gi_ in range(2):
                            if alive[gi_]:
                                try:
                                    next(gens[gi_])
                                except StopIteration:
                                    alive[gi_] = False
                    state_phase(cp, 0)
                    state_phase(cp + 1, 1)

        for lg in range(4):
            if stg in ("st1", "st2", "st3"):
                break
            S.dma("pool", wa_bf[:], lwa_d[:, lg * 4:(lg + 1) * 4, :], [], [wa_bf])
            S.dma("pool", wx_bf[:], lwx_d[:, lg * 4:(lg + 1) * 4, :], [], [wx_bf])
            wu_ = load_w(wbf_in, 6592 + lg * 1024, 512)
            wgt = load_w(wbf_in, 6592 + lg * 1024 + 512, 512) if full else None
            for j in range(4):
                ct = lg * 4 + j
                uc, rg, ig, a1, a2, gx, hs, gl, t1, t2 = tmps[0:10]
                z = zs[0]
                zs_i[0] += 1
                p = proj_tile(wu_, j * 128, 128)
                S.op("pool", "tensor_copy", [ucarry], [z], out=z[:, 0:3], in_=ucarry[:, ct, :])
                S.op("act", "activation", [p], [z], out=z[:, 3:NT + 3], in_=p[:, :], func=AF.Copy)
                S.op("pool", "tensor_copy", [z], [ucarry], out=ucarry[:, ct, :], in_=z[:, NT:NT + 3])
                S.op("dve", "tensor_scalar", [z, cv], [uc], out=uc[:], in0=z[:, 3:NT + 3], scalar1=C("cw", 48 + ct),
                     scalar2=C("cb", ct), op0=ALU.mult, op1=ALU.add)
                for jj in range(3):
                    S.op("dve", "scalar_tensor_tensor", [z, cv, uc], [uc], out=uc[:], in0=z[:, jj:jj + NT],
                         scalar=C("cw", jj * 16 + ct), in1=uc[:], op0=ALU.mult, op1=ALU.add)
                S.op("pool", "tensor_copy", [uc], [ucb], out=ucb[:], in_=uc[:])
                p = next_pA()
                S.op("pe", "matmul", [wa_bf, ucb], [p], out=p[:, :], lhsT=wa_bf[:, j, :], rhs=ucb[:], start=True, stop=True)
                S.op("act", "activation", [p, cv], [rg], out=rg[:], in_=p[:, :], func=AF.Sigmoid, bias=C("ba", ct), scale=1.0)
                p = next_pA()
                S.op("pe", "matmul", [wx_bf, ucb], [p], out=p[:, :], lhsT=wx_bf[:, j, :], rhs=ucb[:], start=True, stop=True)
                S.op("act", "activation", [p, cv], [ig], out=ig[:], in_=p[:, :], func=AF.Sigmoid, bias=C("bx", ct), scale=1.0)
                S.op("act", "activation", [rg, cneg], [a1], out=a1[:], in_=rg[:], func=AF.Exp, scale=cneg[:, ct:ct + 1])
                S.op("act", "activation", [rg, cneg2], [a2], out=a2[:], in_=rg[:], func=AF.Exp, scale=cneg2[:, ct:ct + 1])
                S.op("dve", "tensor_scalar", [a2], [a2], out=a2[:], in0=a2[:], scalar1=1.0, scalar2=None, op0=ALU.min)
                S.op("act", "activation", [a2], [a2], out=a2[:], in_=a2[:], func=AF.Sqrt, bias=1.0, scale=-1.0)
                S.op("dve", "tensor_tensor", [ig, uc], [gx], out=gx[:], in0=ig[:], in1=uc[:], op=ALU.mult)
                S.op("dve", "scalar_tensor_tensor", [gx, bmask, a2], [gx], out=gx[:], in0=gx[:], scalar=bmask[:, b:b + 1],
                     in1=a2[:], op0=ALU.mult, op1=ALU.mult)
                S.op("dve", "tensor_tensor_scan", [a1, gx, hcarry], [hs], out=hs[:], data0=a1[:], data1=gx[:],
                     initial=hcarry[:, ct:ct + 1], op0=ALU.mult, op1=ALU.add)
                S.op("pool", "tensor_copy", [hs], [hcarry], out=hcarry[:, ct:ct + 1], in_=hs[:, NT - 1:NT])
                if full:
                    p = proj_tile(wgt, j * 128, 128)
                    S.op("act", "activation", [p], [gl], out=gl[:], in_=p[:, :], func=AF.Copy)
                    S.op("dve", "tensor_tensor", [gl], [t1], out=t1[:], in0=gl[:], in1=gl[:], op=ALU.mult)
                    S.op("dve", "tensor_scalar", [t1], [t1], out=t1[:], in0=t1[:], scalar1=0.044715, scalar2=1.0,
                         op0=ALU.mult, op1=ALU.add)
                    S.op("dve", "tensor_tensor", [t1, gl], [t1], out=t1[:], in0=t1[:], in1=gl[:], op=ALU.mult)
                    S.op("act", "activation", [t1], [t1], out=t1[:], in_=t1[:], func=AF.Sigmoid, scale=1.5957691216057308)
                    S.op("dve", "tensor_tensor", [t1, gl], [t1], out=t1[:], in0=t1[:], in1=gl[:], op=ALU.mult)
                    S.op("dve", "tensor_tensor", [t1, hs], [hs], out=hs[:], in0=t1[:], in1=hs[:], op=ALU.mult)
                    p = proj_small(10688 + ct * 128)
                    S.op("act", "activation", [p], [t1], out=t1[:], in_=p[:, :], func=AF.Sigmoid)
                    S.op("dve", "tensor_tensor", [t1, hs], [hs], out=hs[:], in0=t1[:], in1=hs[:], op=ALU.mult)
                    p = proj_small(12736 + ct * 128)
                    S.op("act", "activation", [p], [t2], out=t2[:], in_=p[:, :], func=AF.Sigmoid)
                    S.op("dve", "tensor_tensor", [t2, yT], [t2], out=t2[:], in0=t2[:], in1=yT[:, fi, ct, :], op=ALU.mult)
                    S.op("dve", "tensor_tensor", [t2, hs], [yT], out=yT[:, fi, ct, :], in0=t2[:], in1=hs[:], op=ALU.add)

        if dbg is not None and dbg[0] == "yT" and b == nsub - 1:
            for ct in range(16):
                S.op("dve", "tensor_copy", [yT], [tmps[0]], out=tmps[0][:], in_=yT[:, fi, ct, :])
                S.final.append(S.dma("sp", dbg_d[ct * 128:(ct + 1) * 128, :], tmps[0][:], [tmps[0]], []))
        if dbg is not None and dbg[0] == "S" and b == nsub - 1:
            for i in range(16):
                S.final.append(S.dma("sp", dbg_d[i * 128:(i + 1) * 128, 0:128], Sst[i][:], [Sst[i]], []))
            S.final.append(S.dma("sp", dbg_d[0:128, 128:144], hcarry[:], [hcarry], []))

    ms.close()
    S.barrier()
    cs = ExitStack()
    scope[0] = cs
    if dbg is None or dbg[0] == "full":
        res = sb("res", [128, NCH, D])
        hTc = sb("hTc", [128, 16, NT], BF16)
        qT = sb("qT", [128, 16, NT], BF16)
        KTm = sb("KTm", [128, 16, 256], BF16)
        Vm = sb("Vm", [128, 2, D], BF16)
        hidT = sb("hidT", [128, 11, NT], BF16)
        Gb = sb("Gb", [128, D])
        Bb = sb("Bb", [128, D])
        mx = sb("mx", [128, 1])
        ssum = sb("ssum", [128, 1])
        esb = sb("esb", [128, 256])
        pnb = sb("pnb", [128, 256], BF16)
        pTs = sb("pTs", [128, 2, NT], BF16)
        sgt = sb("sgt", [128, NT])

        def ln_tok(tt, gi, dstT):
            r_ = res[:, tt, :]
            ln_stats_ap(r_)
            S.op("dve", "tensor_scalar", [res, mv, rstd], [res], out=r_, in0=r_, scalar1=mv[:, 0:1],
                 scalar2=rstd[:, 0:1], op0=ALU.subtract, op1=ALU.mult)
            S.op("dve", "tensor_tensor", [res, Gb], [res], out=r_, in0=r_, in1=Gb[:], op=ALU.mult)
            S.op("pool", "tensor_tensor", [res, Bb], [res], out=r_, in0=r_, in1=Bb[:], op=ALU.add)
            if dstT is not None:
                S.op("act", "activation", [res], [xn], out=xn[:], in_=r_, func=AF.Copy)

                def ev(pt, k8, kt, tt=tt):
                    eng = "act" if k8 % 2 == 0 else "dve"
                    if eng == "act":
                        S.op("act", "activation", [pt], [dstT], out=dstT[:, kt, tt * 128:(tt + 1) * 128], in_=pt[:, k8, :], func=AF.Copy)
                    else:
                        S.op("dve", "tensor_copy", [pt], [dstT], out=dstT[:, kt, tt * 128:(tt + 1) * 128], in_=pt[:, k8, :])
                transpose16(xn, dstT, tt * 128, ev)

        def ln_stats_ap(ap):
            for c4 in range(4):
                S.op("dve", "bn_stats", [res], [st6], out=st6[:, c4, :], in_=ap[:, c4 * 512:(c4 + 1) * 512])
            S.op("dve", "bn_aggr", [st6], [mv], out=mv[:], in_=st6[:].rearrange("p a b -> p (a b)"))
            S.op("act", "activation", [mv], [rstd], out=rstd[:], in_=mv[:, 1:2], func=AF.Sqrt, bias=LN_EPS, scale=1.0)
            S.op("dve", "reciprocal", [rstd], [rstd], out=rstd[:], in_=rstd[:])

        def load_gb(gi):
            S.dma("sp", Gb[:], bc_d[gi], [], [Gb])
            S.dma("sp", Bb[:], bc_d[gi + 1], [], [Bb])

        def tokmajor_mm(lhsT_buf, lhs_ap_fn, wbuf, nk_total, evac):
            for nb in range(4):
                k0 = 0
                first = True
                while k0 < nk_total:
                    nk = min(16, nk_total - k0)
                    wtile = load_w(wbuf, nb * 512, 512, k0=k0, nk=nk)
                    for tt in range(NCH):
                        for kk_ in range(nk):
                            kt = k0 + kk_
                            S.op("pe", "matmul", [lhsT_buf, wtile], [pM[tt]], out=pM[tt][:, :], lhsT=lhs_ap_fn(kt, tt),
                                 rhs=wtile[:, kk_, 0:512], start=(kt == 0), stop=(kt == nk_total - 1))
                    k0 += nk
                for tt in range(NCH):
                    evac(tt, nb, pM[tt])

        for mt in range(2):
            S.dma("sp", xt[:], mem_d[mt * 128:(mt + 1) * 128, :], [], [xt])
            S.op("act", "activation", [xt], [xn], out=xn[:], in_=xt[:], func=AF.Copy)

            def evm(pt, k8, kt, mt=mt):
                S.op("dve", "tensor_copy", [pt], [qT], out=qT[:, kt, mt * 128:(mt + 1) * 128], in_=pt[:, k8, :])
            transpose16(xn, qT, 0, evm)
        for g4 in range(4):
            wtile = load_w(wbf_k, g4 * 512, 512)
            for j in range(4):
                dt_ = g4 * 4 + j
                p = next_pA()
                for kt in range(16):
                    S.op("pe", "matmul", [wtile, qT], [p], out=p[:, 0:256], lhsT=wtile[:, kt, j * 128:(j + 1) * 128],
                         rhs=qT[:, kt, 0:256], start=(kt == 0), stop=(kt == 15))
                S.op("act", "activation", [p], [KTm], out=KTm[:, dt_, :], in_=p[:, 0:256], func=AF.Copy)
        for nb in range(4):
            wtile = load_w(wbf_v, nb * 512, 512)
            for mt in range(2):
                p = next_pA()
                for kt in range(16):
                    S.op("pe", "matmul", [wtile, qT], [p], out=p[:, :], lhsT=qT[:, kt, mt * 128:(mt + 1) * 128],
                         rhs=wtile[:, kt, 0:512], start=(kt == 0), stop=(kt == 15))
                S.op("act", "activation", [p], [Vm], out=Vm[:, mt, nb * 512:(nb + 1) * 512], in_=p[:, :], func=AF.Copy)

        for fi in range(nfull):
            b = nsub - nfull + fi
            load_gb(2)
            for tt in range(NCH):
                r0 = b * NT + tt * 128
                S.dma("sp", xt[:], x_d[r0:r0 + 128, :], [], [xt])
                ln_stats(xt)
                r_ = res[:, tt, :]
                S.op("dve", "tensor_scalar", [xt, mv, rstd], [res], out=r_, in0=xt[:], scalar1=mv[:, 0:1],
                     scalar2=rstd[:, 0:1], op0=ALU.subtract, op1=ALU.mult)
                S.op("dve", "tensor_tensor", [res, Gb], [res], out=r_, in0=r_, in1=Gb[:], op=ALU.mult)
                S.op("pool", "tensor_tensor", [res, Bb], [res], out=r_, in0=r_, in1=Bb[:], op=ALU.add)

            def ev_res(tt, nb, p):
                cs_ = slice(nb * 512, (nb + 1) * 512)
                S.op("dve", "scalar_tensor_tensor", [res, p], [res], out=res[:, tt, cs_], in0=res[:, tt, cs_], scalar=ALPHA,
                     in1=p[:, :], op0=ALU.mult, op1=ALU.add)
            tokmajor_mm(yT, lambda kt, tt, fi=fi: yT[:, fi, kt, tt * 128:(tt + 1) * 128], wbf_out, 16, ev_res)
            load_gb(4)
            for tt in range(NCH):
                ln_tok(tt, 4, hTc)
            for g4 in range(4):
                wtile = load_w(wbf_q, g4 * 512, 512)
                for j in range(4):
                    dt_ = g4 * 4 + j
                    p = next_pA()
                    for kt in range(16):
                        S.op("pe", "matmul", [wtile, hTc], [p], out=p[:, :], lhsT=wtile[:, kt, j * 128:(j + 1) * 128],
                             rhs=hTc[:, kt, :], start=(kt == 0), stop=(kt == 15))
                    S.op("act", "activation", [p], [qT], out=qT[:, dt_, :], in_=p[:, :], func=AF.Copy, scale=512 ** -0.5)
            for hd_ in range(4):
                for tt in range(NCH):
                    p = next_pA()
                    for d4 in range(4):
                        dt_ = hd_ * 4 + d4
                        S.op("pe", "matmul", [qT, KTm], [p], out=p[:, 0:256], lhsT=qT[:, dt_, tt * 128:(tt + 1) * 128],
                             rhs=KTm[:, dt_, :], start=(d4 == 0), stop=(d4 == 3))
                    S.op("dve", "tensor_reduce", [p], [mx], out=mx[:], in_=p[:, 0:256], axis=mybir.AxisListType.X, op=ALU.max)
                    S.op("dve", "tensor_scalar", [mx], [mx], out=mx[:], in0=mx[:], scalar1=-1.0, scalar2=None, op0=ALU.mult)
                    S.op("act", "activation", [p, mx], [esb, ssum], out=esb[:], in_=p[:, 0:256], func=AF.Exp, bias=mx[:, 0:1],
                         scale=1.0, accum_out=ssum[:])
                    S.op("dve", "reciprocal", [ssum], [ssum], out=ssum[:], in_=ssum[:])
                    S.op("dve", "tensor_scalar", [esb, ssum], [pnb], out=pnb[:], in0=esb[:], scalar1=ssum[:, 0:1], scalar2=None,
                         op0=ALU.mult)
                    for mt in range(2):
                        S.op("pe", "transpose", [pnb, ident_b], [pT[0]], out=pT[0][:, mt, :], in_=pnb[:, mt * 128:(mt + 1) * 128],
                             identity=ident_b[:])
                    S.op("act", "activation", [pT[0]], [pTs], out=pTs[:, :, tt * 128:(tt + 1) * 128], in_=pT[0][:, 0:2, :], func=AF.Copy)
                for d4 in range(4):
                    dt_ = hd_ * 4 + d4
                    p = next_pA()
                    for mt in range(2):
                        S.op("pe", "matmul", [Vm, pTs], [p], out=p[:, :], lhsT=Vm[:, mt, dt_ * 128:(dt_ + 1) * 128],
                             rhs=pTs[:, mt, :], start=(mt == 0), stop=(mt == 1))
                    S.op("act", "activation", [p], [hTc], out=hTc[:, dt_, :], in_=p[:, :], func=AF.Copy)
            tokmajor_mm(hTc, lambda kt, tt: hTc[:, kt, tt * 128:(tt + 1) * 128], wbf_o, 16, ev_res)
            load_gb(6)
            for tt in range(NCH):
                ln_tok(tt, 6, hTc)
            for hh_ in range(4):
                for g4 in range(3):
                    ntile = 4 if g4 < 2 else 3
                    c0 = hh_ * 1408 + g4 * 512
                    wgt_ = load_w(wbf_g, c0, ntile * 128)
                    wut_ = load_w(wbf_u, c0, ntile * 128)
                    for j in range(ntile):
                        ht_ = g4 * 4 + j
                        pg = next_pA()
                        for kt in range(16):
                            S.op("pe", "matmul", [wgt_, hTc], [pg], out=pg[:, :], lhsT=wgt_[:, kt, j * 128:(j + 1) * 128],
                                 rhs=hTc[:, kt, :], start=(kt == 0), stop=(kt == 15))
                        S.op("act", "activation", [pg], [sgt], out=sgt[:], in_=pg[:, :], func=AF.Silu)
                        pu = next_pA()
                        for kt in range(16):
                            S.op("pe", "matmul", [wut_, hTc], [pu], out=pu[:, :], lhsT=wut_[:, kt, j * 128:(j + 1) * 128],
                                 rhs=hTc[:, kt, :], start=(kt == 0), stop=(kt == 15))
                        S.op("dve", "tensor_tensor", [sgt, pu], [hidT], out=hidT[:, ht_, :], in0=sgt[:], in1=pu[:, :], op=ALU.mult)

                def ev_ffn(tt, nb, p, hh_=hh_):
                    cs_ = slice(nb * 512, (nb + 1) * 512)
                    if hh_ == 0:
                        S.op("dve", "scalar_tensor_tensor", [res, p], [res], out=res[:, tt, cs_], in0=res[:, tt, cs_], scalar=ALPHA,
                             in1=p[:, :], op0=ALU.mult, op1=ALU.add)
                    else:
                        S.op("dve", "tensor_tensor", [res, p], [res], out=res[:, tt, cs_], in0=res[:, tt, cs_], in1=p[:, :], op=ALU.add)
                for nb in range(4):
                    for (k0, nk) in [(0, 11)]:
                        wtile = load_w(wbf_d, nb * 512, 512, k0=hh_ * 11 + k0, nk=nk)
                        for tt in range(NCH):
                            for kk_ in range(nk):
                                kt = k0 + kk_
                                S.op("pe", "matmul", [hidT, wtile], [pM[tt]], out=pM[tt][:, :], lhsT=hidT[:, kt, tt * 128:(tt + 1) * 128],
                                     rhs=wtile[:, kk_, 0:512], start=(kt == 0), stop=(kt == 10))
                    for tt in range(NCH):
                        ev_ffn(tt, nb, pM[tt])
            load_gb(8)
            for tt in range(NCH):
                ln_tok(tt, 8, None)
                S.final.append(S.dma("sp", y_d[fi * NT + tt * 128:fi * NT + (tt + 1) * 128, :], res[:, tt, :], [res], []))
    else:
        zt = sb("zt", [128, D])
        S.op("dve", "memset", [], [zt], zt[:], 0.0)
        for i in range(nfull * NCH):
            S.final.append(S.dma("sp", y_d[i * 128:(i + 1) * 128, :], zt[:], [zt], []))

    with nc.Block() as block:
        S.emit(block)
    cs.close()
    es.close()
    return nc


def _perm_cols():
    u0, gl0, rw0 = 0, 2048, 4096
    r0, k0, v0 = rw0, rw0 + 2048, rw0 + 4096
    lo0 = rw0 + 6144
    ga0 = rw0 + 6592
    gb0 = ga0 + 2048
    idx = []
    for hg in range(8):
        idx += list(range(k0 + hg * 256, k0 + hg * 256 + 256))
        idx += list(range(v0 + hg * 256, v0 + hg * 256 + 256))
        idx += list(range(r0 + hg * 256, r0 + hg * 256 + 256))
    idx += list(range(lo0, lo0 + 448))
    for lg in range(4):
        idx += list(range(u0 + lg * 512, u0 + lg * 512 + 512))
        idx += list(range(gl0 + lg * 512, gl0 + lg * 512 + 512))
    idx += list(range(ga0, ga0 + 2048))
    idx += list(range(gb0, gb0 + 2048))
    return np.asarray(idx)


def host_prep(inp, nsub=16, nfull=3, ends=None):
    if ends is None:
        ends = [nsub] * 8
    f = np.float32
    g = {k: np.asarray(v, dtype=f) for k, v in inp.items()}
    perm = _perm_cols()
    w_in = np.ascontiguousarray(g["w_in"][0][:, perm])
    mu_full = np.zeros(NIN, f)
    mu_full[4096:4096 + 6592] = g["rw_mu"][0]
    mu_p = mu_full[perm]
    cvec = np.zeros((128, CM.n), f)

    def put(name, arr2d):
        o = CM.off[name]
        cvec[:, o:o + arr2d.shape[0]] = arr2d.T
    put("lng", g["ln_in_g"].reshape(16, 128))
    put("lnb", g["ln_in_b"].reshape(16, 128))
    put("mu_rw", mu_p[0:6144].reshape(48, 128))
    lo = mu_p[6144:6144 + 448]
    mlo = np.zeros((4, 128), f)
    mlo[0, :96] = lo[0:96]
    mlo[1, :96] = lo[96:192]
    mlo[2] = lo[192:320]
    mlo[3] = lo[320:448]
    put("mu_lo", mlo)
    put("w0", g["rw_w0"][0].reshape(16, 128))
    put("a0", g["rw_a0"][0].reshape(16, 128))
    put("kk", g["rw_kk"][0].reshape(16, 128))
    put("ka", g["rw_ka"][0].reshape(16, 128))
    put("rk", g["rw_rk"][0].reshape(16, 128))
    cw = g["conv_w"][0]
    put("cw", np.stack([cw[j].reshape(16, 128) for j in range(4)], 0).reshape(64, 128))
    put("cb", g["conv_b"][0].reshape(16, 128))
    put("ba", g["lru_ba"][0].reshape(16, 128))
    put("bx", g["lru_bx"][0].reshape(16, 128))
    put("lam", g["lru_lambda"][0].reshape(16, 128))
    bc = np.stack([np.broadcast_to(v.reshape(1, D), (128, D)) for v in
                   [g["rw_gn_g"][0], g["rw_gn_b"][0], g["ln_in_g"], g["ln_in_b"], g["ln1_g"][0], g["ln1_b"][0],
                    g["ln2_g"][0], g["ln2_b"][0], g["ln3_g"][0], g["ln3_b"][0]]], 0).astype(f)
    shared = {
        "w_in": w_in, "w_out": g["w_out"][0], "xa_wq": g["xa_wq"][0], "xa_wk": g["xa_wk"][0], "xa_wv": g["xa_wv"][0],
        "xa_wo": g["xa_wo"][0], "ffn_wg": g["ffn_wg"][0], "ffn_wu": g["ffn_wu"][0], "ffn_wd": g["ffn_wd"][0],
        "mem": g["mem"][0], "cvec": cvec,
        "lru_wa": np.ascontiguousarray(g["lru_wa"][0].transpose(1, 0, 2)),
        "lru_wx": np.ascontiguousarray(g["lru_wx"][0].transpose(1, 0, 2)),
        "rw_wB": g["rw_wB"][0], "rw_aB": g["rw_aB"][0], "rw_gB": g["rw_gB"][0], "bcast": np.ascontiguousarray(bc),
    }
    x = g["x"][0]
    T = nsub * BLK
    maps = []
    for e in ends:
        n_valid = e
        xl = np.zeros((T, D), f)
        xl[T - n_valid * BLK:] = x[0:e * BLK]
        bmk = np.zeros((128, 16), f)
        bmk[:, nsub - n_valid:nsub] = 1.0
        m = dict(shared)
        m["x"] = xl
        m["bmask"] = bmk
        maps.append(m)
    return maps


ENDS = [3, 6, 9, 12, 14, 16]
NFULL = 3


_NC_CACHE = {}


def kernel(**inputs):
    maps = host_prep(inputs, 16, NFULL, ENDS)
    if "nc" not in _NC_CACHE:
        _NC_CACHE["nc"] = build(16, NFULL)
    n = len(ENDS)
    res = run_bass_kernel_spmd(_NC_CACHE["nc"], maps, core_ids=list(range(n)))
    out = np.zeros((16 * BLK, D), np.float32)
    for gsb in range(16):
        for i, e in enumerate(ENDS):
            if e - NFULL <= gsb < e:
                fi = gsb - (e - NFULL)
                y = np.asarray(res.results[i]["y"], dtype=np.float32)
                out[gsb * BLK:(gsb + 1) * BLK] = y[fi * BLK:(fi + 1) * BLK]
                break
    return out.reshape(1, 8192, D)
```

```python
import numpy as np
import ml_dtypes
from contextlib import ExitStack
import concourse.bass as bass
import concourse.mybir as mybir
from concourse.bass_utils import run_bass_kernel_spmd

F32 = mybir.dt.float32
BF16 = mybir.dt.bfloat16
AF = mybir.ActivationFunctionType
ALU = mybir.AluOpType

D = 2048
NIN = 14784
DFF = 5632
BLK = 512
ALPHA = 2 ** 0.25
LN_EPS = 1e-5
GN_EPS = 64e-5
WSC = 0.6065306597126334


class Buf:
    def __init__(self, t, name=""):
        self.t = t
        self.name = name
        self.w = {}
        self.r = {}

    def __getitem__(self, idx):
        return self.t[idx]


class Sched:
    def __init__(self, nc, es):
        self.nc = nc
        self.es = es
        self.engs = {"pe": nc.tensor, "act": nc.scalar, "dve": nc.vector, "pool": nc.gpsimd, "sp": nc.sync}
        self.ops = {k: [] for k in self.engs}
        self.cnt = {k: 0 for k in self.engs}
        self.sems = {}
        for k in ("pe", "act", "dve", "pool"):
            self.sems[k] = es.enter_context(nc.semaphore("sem_" + k))
        self.ndsem = 8
        self.dq = {}
        for q in ("sp", "act", "pool"):
            for i in range(self.ndsem):
                self.sems[("d", q, i)] = es.enter_context(nc.semaphore("dsem_%s_%d" % (q, i)))
            self.dq[q] = {"n": 0, "val": [0] * self.ndsem}
        self.final = []
        self.bar = {}

    def barrier(self):
        allw = {k: self.cnt[k] for k in ("pe", "act", "dve", "pool") if self.cnt[k] > 0}
        for q, st in self.dq.items():
            for i, v in enumerate(st["val"]):
                if v > 0:
                    allw[("d", q, i)] = v
        self.bar = {e: dict(allw) for e in self.ops}

    def _bar(self, eng, waits):
        for k, v in self.bar.pop(eng, {}).items():
            if waits.get(k, 0) < v:
                waits[k] = v

    def _deps(self, reads, writes):
        waits = {}

        def add(tok):
            if tok is None:
                return
            k, v = tok
            if waits.get(k, 0) < v:
                waits[k] = v
        for b in reads:
            for k, v in b.w.items():
                add((k, v))
        for b in writes:
            for k, v in b.w.items():
                add((k, v))
            for k, v in b.r.items():
                add((k, v))
        return waits

    def _mark(self, tok, reads, writes):
        k, v = tok
        for b in reads:
            if b.r.get(k, 0) < v:
                b.r[k] = v
        for b in writes:
            if b.w.get(k, 0) < v:
                b.w[k] = v

    def op(self, eng, method, reads, writes, *args, **kw):
        waits = self._deps(reads, writes)
        self._bar(eng, waits)
        if eng == "pe":
            waits.pop("pe", None)
        self.cnt[eng] += 1
        tok = (eng, self.cnt[eng])
        self._mark(tok, reads, writes)
        self.ops[eng].append((waits, method, args, kw, eng, 1))
        return tok

    def dma(self, q, out, in_, reads, writes, **kw):
        waits = self._deps(reads, writes)
        self._bar(q, waits)
        st = self.dq[q]
        i = st["n"] % self.ndsem
        st["n"] += 1
        key = ("d", q, i)
        if st["val"][i] > 0:
            if waits.get(key, 0) < st["val"][i]:
                waits[key] = st["val"][i]
        st["val"][i] += 16
        tok = (key, st["val"][i])
        self._mark(tok, reads, writes)
        kw = dict(kw)
        kw["out"] = out
        kw["in_"] = in_
        self.ops[q].append((waits, "dma_start", (), kw, key, 16))
        return tok

    def emit(self, block):
        nc = self.nc
        final = {}
        for k, v in self.final:
            if final.get(k, 0) < v:
                final[k] = v

        def mk(name):
            def body(e):
                seen = {}
                for waits, method, args, kw, semkey, inc in self.ops[name]:
                    for k, v in waits.items():
                        if seen.get(k, 0) < v:
                            e.wait_ge(self.sems[k], v)
                            seen[k] = v
                    ins = getattr(e, method)(*args, **kw)
                    ins.then_inc(self.sems[semkey], inc)
                if name == "sp":
                    for k, v in final.items():
                        e.wait_ge(self.sems[k], v)
            return body
        block.sync(mk("sp"))
        block.tensor(mk("pe"))
        block.scalar(mk("act"))
        block.vector(mk("dve"))
        block.gpsimd(mk("pool"))


class CMap:
    def __init__(self):
        self.off = {}
        self.n = 0

    def add(self, name, ncol):
        self.off[name] = self.n
        self.n += ncol


def make_cmap():
    cm = CMap()
    for nm, n in [("lng", 16), ("lnb", 16), ("mu_rw", 48), ("mu_lo", 4), ("w0", 16), ("a0", 16), ("kk", 16),
                  ("ka", 16), ("rk", 16), ("cw", 64), ("cb", 16), ("ba", 16), ("bx", 16), ("lam", 16)]:
        cm.add(nm, n)
    return cm


CM = make_cmap()


def build(nsub=16, nfull=2, dbg=None):
    nc = bass.Bass("TRN2", target_bir_lowering=False)
    es = ExitStack()
    T = nsub * BLK
    NT = BLK
    NCH = NT // 128

    def din(name, shape, dt=F32):
        return nc.dram_tensor(name, list(shape), dt, kind="ExternalInput").ap()

    small = dbg is not None and dbg[0] != "full"
    x_d = din("x", [T, D])
    win_d = din("w_in", [D, NIN])
    wout_d = din("w_out", [128, 128] if small else [D, D])
    wq_d = din("xa_wq", [128, 128] if small else [D, D])
    wk_d = din("xa_wk", [128, 128] if small else [D, D])
    wv_d = din("xa_wv", [128, 128] if small else [D, D])
    wo_d = din("xa_wo", [128, 128] if small else [D, D])
    wg_d = din("ffn_wg", [128, 128] if small else [D, DFF])
    wu_d = din("ffn_wu", [128, 128] if small else [D, DFF])
    wd_d = din("ffn_wd", [128, 128] if small else [DFF, D])
    mem_d = din("mem", [256, D])
    cvec_d = din("cvec", [128, CM.n])
    lwa_d = din("lru_wa", [128, 16, 128])
    lwx_d = din("lru_wx", [128, 16, 128])
    wB_d = din("rw_wB", [96, D])
    aB_d = din("rw_aB", [96, D])
    gB_d = din("rw_gB", [256, D])
    bc_d = din("bcast", [10, 128, D])
    bmask_d = din("bmask", [128, 16])
    y_d = nc.dram_tensor("y", [nfull * NT, D], F32, kind="ExternalOutput").ap()
    dbg_d = None
    if dbg is not None:
        dbg_d = nc.dram_tensor("dbg", list(dbg[1]), F32, kind="ExternalOutput").ap()

    wbf_in = Buf(win_d, "w_in")
    wbf_out = Buf(wout_d, "w_out")
    wbf_q = Buf(wq_d, "wq")
    wbf_k = Buf(wk_d, "wk")
    wbf_v = Buf(wv_d, "wv")
    wbf_o = Buf(wo_d, "wo")
    wbf_g = Buf(wg_d, "wg")
    wbf_u = Buf(wu_d, "wu")
    wbf_d = Buf(wd_d, "wd")

    S = Sched(nc, es)
    ms = ExitStack()
    scope = [es]

    def sb(name, shape, dt=F32):
        return Buf(scope[0].enter_context(nc.sbuf_tensor("s_" + name, list(shape), dt)), name)

    def ps(name, shape, dt=F32):
        return Buf(scope[0].enter_context(nc.psum_tensor("p_" + name, list(shape), dt)), name)

    cv = sb("cv", [128, CM.n])
    bmask = sb("bmask", [128, 16])
    ident_b = sb("ident_b", [128, 128], BF16)
    ident_f = sb("ident_f", [128, 128])
    ones_f = sb("ones_f", [128, 128])
    yT = sb("yT", [128, nfull, 16, NT], BF16)
    xt = sb("xt", [128, D])
    xn = sb("xn", [128, D], BF16)
    st6 = sb("st6", [128, 4, 6])
    mv = sb("mv", [128, 2])
    rstd = sb("rstd", [128, 1])
    wt = [sb("wt%d" % i, [128, 16, 512], BF16) for i in range(2)]
    pT0 = ps("pT0", [128, 8, 128], BF16)
    pT = [pT0, pT0]
    pA = [ps("pA%d" % i, [128, 512]) for i in range(1)]
    pM = [ps("pM%d" % i, [128, 512]) for i in range(6)]
    pA = [pA[0], pM[5], pM[4]]

    def C(name, j=0, n=1, rows=128):
        o = CM.off[name] + j
        return cv[0:rows, o:o + n]

    S.dma("sp", cv[:], cvec_d, [], [cv])
    S.dma("sp", bmask[:], bmask_d, [], [bmask])
    S.op("dve", "memset", [], [ones_f], ones_f[:], 1.0)
    S.op("pool", "memset", [], [ident_f], ident_f[:], 0.0)
    S.op("pool", "affine_select", [ident_f], [ident_f], out=ident_f[:], in_=ident_f[:], pattern=[[-1, 128]],
         compare_op=ALU.not_equal, fill=1.0, base=0, channel_multiplier=1)
    S.op("dve", "tensor_copy", [ident_f], [ident_b], out=ident_b[:], in_=ident_f[:])

    wt_i = [0]

    def load_w(wbuf, c0, ncol, k0=0, nk=16, q="pool"):
        t = wt[wt_i[0] % 2]
        wt_i[0] += 1
        for ka in range(0, nk, 4):
            kb = min(nk, ka + 4)
            S.dma(q, t[:, ka:kb, 0:ncol],
                  wbuf.t[(k0 + ka) * 128:(k0 + kb) * 128, c0:c0 + ncol].rearrange("(k p) c -> p k c", p=128), [], [t])
        return t

    pa_i = [0]

    def next_pA():
        p = pA[pa_i[0] % len(pA)]
        pa_i[0] += 1
        return p

    def ln_stats(src):
        for c4 in range(4):
            S.op("dve", "bn_stats", [src], [st6], out=st6[:, c4, :], in_=src[:, c4 * 512:(c4 + 1) * 512])
        S.op("dve", "bn_aggr", [st6], [mv], out=mv[:], in_=st6[:].rearrange("p a b -> p (a b)"))
        S.op("act", "activation", [mv], [rstd], out=rstd[:], in_=mv[:, 1:2], func=AF.Sqrt, bias=LN_EPS, scale=1.0)
        S.op("dve", "reciprocal", [rstd], [rstd], out=rstd[:], in_=rstd[:])

    def transpose16(src_bf, dstT, tok0, evac):
        for half in range(2):
            pt = pT[half]
            for k8 in range(8):
                kt = half * 8 + k8
                S.op("pe", "transpose", [src_bf, ident_b], [pt], out=pt[:, k8, :], in_=src_bf[:, kt * 128:(kt + 1) * 128],
                     identity=ident_b[:])
            for k8 in range(8):
                evac(pt, k8, half * 8 + k8)

    scope[0] = ms
    gm = sb("gm", [128, 16, 16])
    bm = sb("bm", [128, 16, 16])
    omk = sb("omk", [128, 16])
    omu = sb("omu", [128, 52])
    cneg = sb("cneg", [128, 16])
    cneg2 = sb("cneg2", [128, 16])
    mSU4 = sb("mSU4", [128, 512], BF16)
    mIU4 = sb("mIU4", [128, 512], BF16)
    mSL4 = sb("mSL4", [128, 512], BF16)
    id4 = sb("id4", [128, 512], BF16)
    bones = sb("bones", [128, 128], BF16)
    hsel = sb("hsel", [128, 2], BF16)
    hm = sb("hm", [128, 2])
    wa_bf = sb("wa_bf", [128, 4, 128], BF16)
    wx_bf = sb("wx_bf", [128, 4, 128], BF16)
    wB_bf = sb("wB_bf", [96, 128], BF16)
    aB_bf = sb("aB_bf", [96, 128], BF16)
    gB_bf = sb("gB_bf", [128, 2, 256], BF16)
    carry = sb("carry", [128, 64])
    ucarry = sb("ucarry", [128, 16, 3])
    hcarry = sb("hcarry", [128, 16])
    Sst = [sb("Sst%d" % i, [128, 128]) for i in range(16)]
    Sbf = [sb("Sbf%d" % i, [128, 128], BF16) for i in range(16)]
    hT = sb("hT", [128, 16, NT], BF16)


    tmpm = sb("tmpm", [128, 128])
    for (msk, pat, cmp_, cm_) in [(mSU4, [[1, 128]], ALU.is_gt, -1), (mIU4, [[1, 128]], ALU.is_ge, -1),
                                  (mSL4, [[-1, 128]], ALU.is_gt, 1)]:
        S.op("pool", "affine_select", [ones_f], [tmpm], out=tmpm[:], in_=ones_f[:], pattern=pat,
             compare_op=cmp_, fill=0.0, base=0, channel_multiplier=cm_)
        for h in range(4):
            S.op("dve", "tensor_copy", [tmpm], [msk], out=msk[:, h * 128:(h + 1) * 128], in_=tmpm[:])
    for h in range(4):
        S.op("dve", "tensor_copy", [ident_f], [id4], out=id4[:, h * 128:(h + 1) * 128], in_=ident_f[:])
    S.op("pool", "memset", [], [bones], bones[:], 0.0)
    S.op("pool", "memset", [bones], [bones], bones[0:64, 0:64], 1.0)
    S.op("pool", "memset", [bones], [bones], bones[64:128, 64:128], 1.0)
    S.op("pool", "memset", [], [hsel], hsel[:], 0.0)
    S.op("pool", "memset", [hsel], [hsel], hsel[0:64, 0:1], 1.0)
    S.op("pool", "memset", [hsel], [hsel], hsel[64:128, 1:2], 1.0)
    S.op("pool", "memset", [], [hm], hm[:], 0.0)
    S.op("pool", "memset", [hm], [hm], hm[0:64, 0:1], 1.0)
    S.op("pool", "memset", [hm], [hm], hm[64:128, 1:2], 1.0)
    S.op("pool", "memset", [], [carry], carry[:], 0.0)
    S.op("pool", "memset", [], [ucarry], ucarry[:], 0.0)
    S.op("pool", "memset", [], [hcarry], hcarry[:], 0.0)
    for i in range(16):
        S.op("pool", "memset", [], [Sst[i]], Sst[i][:], 0.0)
        S.op("pool", "memset", [], [Sbf[i]], Sbf[i][:], 0.0)
    for b in range(nsub):
        S.op("dve", "tensor_scalar", [cv, bmask], [gm], out=gm[:, b, :], in0=C("lng", 0, 16), scalar1=bmask[:, b:b + 1],
             scalar2=None, op0=ALU.mult)
        S.op("dve", "tensor_scalar", [cv, bmask], [bm], out=bm[:, b, :], in0=C("lnb", 0, 16), scalar1=bmask[:, b:b + 1],
             scalar2=None, op0=ALU.mult)
    S.op("dve", "tensor_scalar", [cv], [omk], out=omk[:], in0=C("ka", 0, 16), scalar1=-1.0, scalar2=1.0,
         op0=ALU.mult, op1=ALU.add)
    assert CM.off["mu_lo"] == CM.off["mu_rw"] + 48
    S.op("dve", "tensor_scalar", [cv], [omu], out=omu[:], in0=C("mu_rw", 0, 52), scalar1=-1.0, scalar2=1.0,
         op0=ALU.mult, op1=ALU.add)
    S.op("act", "activation", [cv], [cneg], out=cneg[:], in_=C("lam", 0, 16), func=AF.Exp, scale=-1.0)
    S.op("act", "activation", [cneg], [cneg], out=cneg[:], in_=cneg[:], func=AF.Ln, bias=1.0, scale=1.0)
    S.op("dve", "tensor_scalar", [cneg], [cneg2], out=cneg2[:], in0=cneg[:], scalar1=-16.0, scalar2=None, op0=ALU.mult)
    S.op("dve", "tensor_scalar", [cneg], [cneg], out=cneg[:], in0=cneg[:], scalar1=-8.0, scalar2=None, op0=ALU.mult)

    NTMP = 10
    tmps = [sb("tmp%d" % i, [128, NT]) for i in range(NTMP)]
    zs = [sb("zs%d" % i, [128, NT + 3]) for i in range(1)]
    zs_i = [0]
    twlo = sb("twlo", [96, NT], BF16)
    alo = sb("alo", [96, NT], BF16)
    sg = sb("sg", [128, 2, NT], BF16)
    ktl = [sb("ktl%d" % q, [128, NT], BF16) for q in range(2)]
    btl = [sb("btl%d" % q, [128, NT], BF16) for q in range(2)]
    atl = [sb("atl%d" % q, [128, NT], BF16) for q in range(4)]
    rtl = [sb("rtl%d" % q, [128, NT], BF16) for q in range(4)]
    vbf = [sb("vbf%d" % q, [128, NT], BF16) for q in range(2)]
    rkk = [sb("rkk%d" % q, [128, NT], BF16) for q in range(2)]
    sqb = sb("sqb", [128, NT], BF16)
    ucb = sqb
    gam = [sb("gam%d" % q, [128, NCH]) for q in range(2)]
    KTs = sb("KTs", [128, NCH, 2, 128], BF16)
    BTs = sb("BTs", [128, NCH, 2, 128], BF16)
    VTs = sb("VTs", [128, NCH, 2, 128], BF16)
    Pb_s = [[sb("Pb%d_%d" % (st, i), [128, 512], BF16) for i in range(2)] for st in range(2)]
    PTb_s = [[sb("PTb%d_%d" % (st, i), [128, 512], BF16) for i in range(2)] for st in range(2)]
    Tb_s = [[sb("Tb%d_%d" % (st, i), [128, 512], BF16) for i in range(2)] for st in range(2)]
    Mka_s = [sb("Mka%d" % st, [128, 512], BF16) for st in range(2)]
    Mkr_s = [sb("Mkr%d" % st, [128, 512], BF16) for st in range(2)]
    Mbr_s = [sb("Mbr%d" % st, [128, 512], BF16) for st in range(2)]
    W1s = sb("W1s", [128, 256], BF16)
    UTs = sb("UTs", [128, 256], BF16)
    Stmp = tmps[0]
    gnG = sb("gnG", [128, 256])
    gnB = sb("gnB", [128, 256])
    gst = sb("gst", [128, 4, 6])
    gmv = sb("gmv", [128, 4, 2])
    grs = sb("grs", [128, 4])
    bsum = sb("bsum", [128, 4])
    onb = tmps[1]
    ofb = tmps[2]
    ybb = sb("ybb", [128, 256], BF16)

    wsm = [sb("wsm%d" % i, [128, 8, 128], BF16) for i in range(1)]

    def proj_small(c0):
        t = wsm[0]
        p = next_pA()
        for hf in range(2):
            for ka in range(0, 8, 4):
                k0 = hf * 8 + ka
                S.dma("pool", t[:, ka:ka + 4, :], wbf_in.t[k0 * 128:(k0 + 4) * 128, c0:c0 + 128].rearrange("(k p) c -> p k c", p=128),
                      [], [t])
            for k8 in range(8):
                kt = hf * 8 + k8
                S.op("pe", "matmul", [t, hT], [p], out=p[:, :], lhsT=t[:, k8, :], rhs=hT[:, kt, :], start=(kt == 0), stop=(kt == 15))
        return p

    def shifted(p, M, cid, mucol, out_ap, out_buf, post=None):
        z = zs[0]
        zs_i[0] += 1
        d = tmps[NTMP - 1]
        S.op("pool", "tensor_copy", [carry], [z], out=z[0:M, 0:1], in_=carry[0:M, cid:cid + 1])
        S.op("act", "activation", [p], [z], out=z[0:M, 1:NT + 1], in_=p[0:M, 0:NT], func=AF.Copy)
        S.op("act", "activation", [p, omu], [d], out=d[0:M, :], in_=p[0:M, 0:NT], func=AF.Identity, scale=omu[0:M, cid:cid + 1])
        S.op("pool", "tensor_copy", [z], [carry], out=carry[0:M, cid:cid + 1], in_=z[0:M, NT:NT + 1])
        S.op("dve", "scalar_tensor_tensor", [d, z, cv], [out_buf], out=out_ap, in0=z[0:M, 0:NT], scalar=mucol,
             in1=d[0:M, :], op0=ALU.mult, op1=ALU.add)

    def proj_tile(wtile, cofs, M):
        p = next_pA()
        for kt in range(16):
            S.op("pe", "matmul", [wtile, hT], [p], out=p[0:M, :], lhsT=wtile[:, kt, cofs:cofs + M],
                 rhs=hT[:, kt, :], start=(kt == 0), stop=(kt == 15))
        return p

    def dump(buf, ap, rows):
        if dbg_d is not None:
            S.final.append(S.dma("sp", dbg_d[0:rows, :], ap, [buf], []))

    for b in range(nsub):
        full = (b >= nsub - nfull)
        fi = b - (nsub - nfull)
        for tt in range(NCH):
            r0 = b * NT + tt * 128
            S.dma("sp", xt[:], x_d[r0:r0 + 128, :], [], [xt])
            ln_stats(xt)
            S.op("dve", "tensor_scalar", [xt, mv, rstd], [xn], out=xn[:], in0=xt[:], scalar1=mv[:, 0:1],
                 scalar2=rstd[:, 0:1], op0=ALU.subtract, op1=ALU.mult)

            def ev(pt, k8, kt, tt=tt, b=b):
                if k8 % 2 == 0:
                    S.op("act", "activation", [pt, gm, bm], [hT], out=hT[:, kt, tt * 128:(tt + 1) * 128], in_=pt[:, k8, :],
                         func=AF.Identity, scale=gm[:, b, kt:kt + 1], bias=bm[:, b, kt:kt + 1])
                else:
                    S.op("dve", "tensor_scalar", [pt, gm, bm], [hT], out=hT[:, kt, tt * 128:(tt + 1) * 128],
                         in0=pt[:, k8, :], scalar1=gm[:, b, kt:kt + 1], scalar2=bm[:, b, kt:kt + 1],
                         op0=ALU.mult, op1=ALU.add)
            transpose16(xn, hT, tt * 128, ev)

        wl = load_w(wbf_in, 6144, 448)
        t0 = tmps[0]
        p = proj_tile(wl, 0, 96)
        shifted(p, 96, 48, C("mu_lo", 0, 1, 96), t0[0:96, :], t0)
        S.op("act", "activation", [t0], [twlo], out=twlo[:], in_=t0[0:96, :], func=AF.Tanh)
        p = proj_tile(wl, 96, 96)
        shifted(p, 96, 49, C("mu_lo", 1, 1, 96), alo[:], alo)
        precarry = (b == nsub - nfull - 1)
        if full:
            for j in range(2):
                p = proj_tile(wl, 192 + j * 128, 128)
                shifted(p, 128, 50 + j, C("mu_lo", 2 + j), t0[:], t0)
                S.op("act", "activation", [t0], [sg], out=sg[:, j, :], in_=t0[:], func=AF.Sigmoid)
        elif precarry:
            for j in range(2):
                p = proj_tile(wl, 192 + j * 128, 128)
                S.op("act", "activation", [p], [carry], out=carry[:, 50 + j:51 + j], in_=p[:, NT - 1:NT], func=AF.Copy)

        stg = dbg[0] if dbg is not None else None
        for hg in range(8):
            if stg in ("st1", "st4") or (stg in ("st2", "st3") and hg > 0):
                break
            wkv = load_w(wbf_in, hg * 768, 512)
            wr = load_w(wbf_in, hg * 768 + 512, 256) if (full or precarry) else None
            if precarry:
                for q in range(2):
                    p = proj_tile(wr, q * 128, 128)
                    S.op("act", "activation", [p], [carry], out=carry[:, hg * 6 + 4 + q:hg * 6 + 5 + q], in_=p[:, NT - 1:NT],
                         func=AF.Copy)
            if full:
                S.dma("pool", gB_bf[:], gB_d[:, hg * 256:(hg + 1) * 256].rearrange("(k p) c -> p k c", p=128), [], [gB_bf])
                S.dma("sp", gnG[:], bc_d[0, :, hg * 256:(hg + 1) * 256], [], [gnG])
                S.dma("sp", gnB[:], bc_d[1, :, hg * 256:(hg + 1) * 256], [], [gnB])
            for q in range(2):
                pt_ = hg * 2 + q
                kp, av, sv, cum, e1, e2, kkn, kf, rp = tmps[0:9]
                rn = e2
                bb = av
                p = proj_tile(wkv, q * 128, 128)
                shifted(p, 128, hg * 6 + q, C("mu_rw", hg * 6 + q), kp[:], kp)
                p = proj_tile(wkv, 256 + q * 128, 128)
                shifted(p, 128, hg * 6 + 2 + q, C("mu_rw", hg * 6 + 2 + q), vbf[q][:], vbf[q])
                if full:
                    p = proj_tile(wr, q * 128, 128)
                    shifted(p, 128, hg * 6 + 4 + q, C("mu_rw", hg * 6 + 4 + q), rp[:], rp)
                S.dma("pool", wB_bf[:], wB_d[:, pt_ * 128:(pt_ + 1) * 128], [], [wB_bf])
                S.dma("pool", aB_bf[:], aB_d[:, pt_ * 128:(pt_ + 1) * 128], [], [aB_bf])
                p = next_pA()
                S.op("pe", "matmul", [wB_bf, twlo], [p], out=p[:, :], lhsT=wB_bf[0:96, :],
                     rhs=twlo[0:96, :], start=True, stop=True)
                S.op("act", "activation", [p, cv], [sv], out=sv[:], in_=p[:, :], func=AF.Sigmoid, bias=C("w0", pt_), scale=1.0)
                p = next_pA()
                S.op("pe", "matmul", [aB_bf, alo], [p], out=p[:, :], lhsT=aB_bf[0:96, :],
                     rhs=alo[0:96, :], start=True, stop=True)
                S.op("act", "activation", [p, cv], [av], out=av[:], in_=p[:, :], func=AF.Sigmoid, bias=C("a0", pt_), scale=1.0)
                S.op("act", "activation", [kp, cv], [sqb], out=sqb[:], in_=kp[:], func=AF.Square, scale=C("kk", pt_))
                p = next_pA()
                S.op("pe", "matmul", [bones, sqb], [p], out=p[:, :], lhsT=bones[:], rhs=sqb[:], start=True, stop=True)
                S.op("act", "activation", [p], [rn], out=rn[:], in_=p[:, :], func=AF.Sqrt)
                S.op("dve", "tensor_scalar", [rn], [rn], out=rn[:], in0=rn[:], scalar1=1e-12, scalar2=None, op0=ALU.max)
                S.op("dve", "reciprocal", [rn], [rn], out=rn[:], in_=rn[:])
                S.op("dve", "scalar_tensor_tensor", [kp, cv, rn], [kkn], out=kkn[:], in0=kp[:], scalar=C("kk", pt_),
                     in1=rn[:], op0=ALU.mult, op1=ALU.mult)
                S.op("dve", "tensor_scalar", [av, cv, omk], [e1], out=e1[:], in0=av[:], scalar1=C("ka", pt_),
                     scalar2=omk[:, pt_:pt_ + 1], op0=ALU.mult, op1=ALU.add)
                S.op("dve", "tensor_tensor", [kp, e1], [kf], out=kf[:], in0=kp[:], in1=e1[:], op=ALU.mult)
                S.op("pool", "tensor_tensor", [kkn, av], [bb], out=bb[:], in0=kkn[:], in1=av[:], op=ALU.mult)
                if full:
                    S.op("dve", "scalar_tensor_tensor", [rp, cv, kf], [rkk[q]], out=rkk[q][:], in0=rp[:], scalar=C("rk", pt_),
                         in1=kf[:], op0=ALU.mult, op1=ALU.mult)
                for c in range(NCH):
                    S.op("dve", "tensor_tensor_scan", [sv, ones_f], [cum], out=cum[:, c * 128:(c + 1) * 128],
                         data0=ones_f[:, 0:128], data1=sv[:, c * 128:(c + 1) * 128], initial=0.0, op0=ALU.mult, op1=ALU.add)
                S.op("act", "activation", [cum], [gam[q]], out=gam[q][:],
                     in_=cum[:].rearrange("p (c t) -> p c t", t=128)[:, :, 127], func=AF.Exp, scale=-WSC)
                S.op("act", "activation", [cum], [e1], out=e1[:], in_=cum[:], func=AF.Exp, scale=WSC)
                S.op("dve", "tensor_tensor", [kf, e1], [ktl[q]], out=ktl[q][:], in0=kf[:], in1=e1[:], op=ALU.mult)
                S.op("dve", "tensor_tensor", [bb, e1], [btl[q]], out=btl[q][:], in0=bb[:], in1=e1[:], op=ALU.mult)
                if full:
                    S.op("act", "activation", [cum], [e2], out=e2[:], in_=cum[:], func=AF.Exp, scale=-WSC)
                    for hh in range(2):
                        S.op("dve", "scalar_tensor_tensor", [rp, hm, e2], [rtl[2 * q + hh]], out=rtl[2 * q + hh][:], in0=rp[:],
                             scalar=hm[:, hh:hh + 1], in1=e2[:], op0=ALU.mult, op1=ALU.mult)
                S.op("dve", "tensor_tensor", [cum, sv], [cum], out=cum[:], in0=cum[:], in1=sv[:], op=ALU.subtract)
                S.op("act", "activation", [cum], [e2], out=e2[:], in_=cum[:], func=AF.Exp, scale=-WSC)
                S.op("dve", "tensor_scalar", [e2], [e2], out=e2[:], in0=e2[:], scalar1=-1.0, scalar2=None, op0=ALU.mult)
                for hh in range(2):
                    S.op("dve", "scalar_tensor_tensor", [kkn, hm, e2], [atl[2 * q + hh]], out=atl[2 * q + hh][:], in0=kkn[:],
                         scalar=hm[:, hh:hh + 1], in1=e2[:], op0=ALU.mult, op1=ALU.mult)
                for (src, dst, pti) in [(ktl[q], KTs, 0), (btl[q], BTs, 1), (vbf[q], VTs, 0)]:
                    ptt = pT[pti]
                    for c in range(NCH):
                        S.op("pe", "transpose", [src, ident_b], [ptt], out=ptt[:, c, :], in_=src[:, c * 128:(c + 1) * 128],
                             identity=ident_b[:])
                    S.op("act", "activation", [ptt], [dst], out=dst[:, :, q, :], in_=ptt[:, 0:NCH, :], func=AF.Copy)

            def hd(h, hg=hg):
                q_, hh = h // 2, h % 2
                return q_, hg * 2 + q_, hh

            def t_phase(c, st, full=full):
                Cc = slice(c * 128, (c + 1) * 128)
                b0, b1, b2 = pM[3 * st], pM[3 * st + 1], pM[3 * st + 2]
                Pb, PTb, Tb = Pb_s[st], PTb_s[st], Tb_s[st]
                for h in range(4):
                    q_, pt_, hh = hd(h)
                    S.op("pe", "matmul", [btl[q_], atl[h]], [b1], out=b1[:, h * 128:(h + 1) * 128],
                         lhsT=btl[q_][:, Cc], rhs=atl[h][:, Cc], start=True, stop=True)
                for h in range(4):
                    q_, pt_, hh = hd(h)
                    S.op("pe", "matmul", [btl[q_], atl[h]], [b2], out=b2[:, h * 128:(h + 1) * 128],
                         lhsT=atl[h][:, Cc], rhs=btl[q_][:, Cc], start=True, stop=True)
                for h in range(4):
                    q_, pt_, hh = hd(h)
                    S.op("pe", "matmul", [ktl[q_], atl[h]], [b0], out=b0[:, h * 128:(h + 1) * 128],
                         lhsT=ktl[q_][:, Cc], rhs=atl[h][:, Cc], start=True, stop=True)
                yield
                S.op("dve", "tensor_tensor", [b1, mSU4], [Pb[0]], out=Pb[0][:], in0=b1[:, :], in1=mSU4[:], op=ALU.mult)
                S.op("dve", "tensor_tensor", [b2, mSL4], [PTb[0]], out=PTb[0][:], in0=b2[:, :], in1=mSL4[:], op=ALU.mult)
                S.op("pool", "tensor_tensor", [Pb[0], id4], [Tb[0]], out=Tb[0][:], in0=Pb[0][:], in1=id4[:], op=ALU.add)
                S.op("dve", "tensor_tensor", [b0, mSU4], [Mka_s[st]], out=Mka_s[st][:], in0=b0[:, :], in1=mSU4[:], op=ALU.mult)
                yield
                cur = 0
                tcur = 0
                for k in range(6):
                    nxt = 1 - cur
                    last = (k == 5)
                    for h in range(4):
                        hs_ = slice(h * 128, (h + 1) * 128)
                        S.op("pe", "matmul", [PTb[cur], Pb[cur]], [b1], out=b1[:, hs_], lhsT=Pb[cur][:, hs_],
                             rhs=PTb[cur][:, hs_], start=True, stop=True)
                    if not last:
                        for h in range(4):
                            hs_ = slice(h * 128, (h + 1) * 128)
                            S.op("pe", "matmul", [PTb[cur], Pb[cur]], [b0], out=b0[:, hs_], lhsT=PTb[cur][:, hs_],
                                 rhs=Pb[cur][:, hs_], start=True, stop=True)
                    yield
                    S.op("act", "activation", [b1], [PTb[nxt]], out=PTb[nxt][:], in_=b1[:, :], func=AF.Copy)
                    if not last:
                        S.op("dve", "tensor_copy", [b0], [Pb[nxt]], out=Pb[nxt][:], in_=b0[:, :])
                    yield
                    for h in range(4):
                        hs_ = slice(h * 128, (h + 1) * 128)
                        S.op("pe", "matmul", [PTb[nxt], Tb[tcur]], [b2], out=b2[:, hs_], lhsT=PTb[nxt][:, hs_],
                             rhs=Tb[tcur][:, hs_], start=True, stop=True)
                    yield
                    S.op("dve", "tensor_tensor", [b2, Tb[tcur]], [Tb[1 - tcur]], out=Tb[1 - tcur][:], in0=b2[:, :],
                         in1=Tb[tcur][:], op=ALU.add)
                    tcur = 1 - tcur
                    cur = nxt
                    yield
                assert tcur == 0
                if full:
                    for h in range(4):
                        q_, pt_, hh = hd(h)
                        S.op("pe", "matmul", [ktl[q_], rtl[h]], [b0], out=b0[:, h * 128:(h + 1) * 128],
                             lhsT=ktl[q_][:, Cc], rhs=rtl[h][:, Cc], start=True, stop=True)
                    S.op("dve", "tensor_tensor", [b0, mIU4], [Mkr_s[st]], out=Mkr_s[st][:], in0=b0[:, :], in1=mIU4[:], op=ALU.mult)
                    yield
                    for h in range(4):
                        q_, pt_, hh = hd(h)
                        S.op("pe", "matmul", [btl[q_], rtl[h]], [b1], out=b1[:, h * 128:(h + 1) * 128],
                             lhsT=btl[q_][:, Cc], rhs=rtl[h][:, Cc], start=True, stop=True)
                    S.op("dve", "tensor_tensor", [b1, mIU4], [Mbr_s[st]], out=Mbr_s[st][:], in0=b1[:, :], in1=mIU4[:], op=ALU.mult)
                    yield

            def state_phase(c, st, full=full, hg=hg):
                Cc = slice(c * 128, (c + 1) * 128)
                b0, b1, b2 = pM[3 * st], pM[3 * st + 1], pM[3 * st + 2]
                Tf = Tb_s[st][0]
                Mka, Mkr, Mbr = Mka_s[st], Mkr_s[st], Mbr_s[st]
                for h in range(4):
                    q_, pt_, hh = hd(h)
                    vs = slice(64 * hh, 64 * hh + 64)
                    S.op("pe", "matmul", [atl[h], Sbf[pt_]], [b0], out=b0[:, h * 64:(h + 1) * 64], lhsT=atl[h][:, Cc],
                         rhs=Sbf[pt_][:, vs], start=True, stop=False)
                    S.op("pe", "matmul", [Mka, VTs], [b0], out=b0[:, h * 64:(h + 1) * 64], lhsT=Mka[:, h * 128:(h + 1) * 128],
                         rhs=VTs[:, c, q_, vs], start=False, stop=True)
                S.op("act", "activation", [b0], [W1s], out=W1s[:], in_=b0[:, 0:256], func=AF.Copy)
                for h in range(4):
                    S.op("pe", "matmul", [Tf, W1s], [b1], out=b1[:, h * 64:(h + 1) * 64], lhsT=Tf[:, h * 128:(h + 1) * 128],
                         rhs=W1s[:, h * 64:(h + 1) * 64], start=True, stop=True)
                S.op("act", "activation", [b1], [UTs], out=UTs[:], in_=b1[:, 0:256], func=AF.Copy)
                if full:
                    for h in range(4):
                        q_, pt_, hh = hd(h)
                        vs = slice(64 * hh, 64 * hh + 64)
                        os_ = slice(h * 64, (h + 1) * 64)
                        S.op("pe", "matmul", [rtl[h], Sbf[pt_]], [b2], out=b2[:, os_], lhsT=rtl[h][:, Cc],
                             rhs=Sbf[pt_][:, vs], start=True, stop=False)
                        S.op("pe", "matmul", [Mkr, VTs], [b2], out=b2[:, os_], lhsT=Mkr[:, h * 128:(h + 1) * 128],
                             rhs=VTs[:, c, q_, vs], start=False, stop=False)
                        S.op("pe", "matmul", [Mbr, UTs], [b2], out=b2[:, os_], lhsT=Mbr[:, h * 128:(h + 1) * 128],
                             rhs=UTs[:, os_], start=False, stop=True)
                for q_ in range(2):
                    qs = slice(q_ * 128, (q_ + 1) * 128)
                    ss = slice(256 + q_ * 128, 384 + q_ * 128)
                    S.op("pe", "matmul", [KTs, VTs], [b1], out=b1[:, ss], lhsT=KTs[:, c, q_, :], rhs=VTs[:, c, q_, :],
                         start=True, stop=False)
                    S.op("pe", "matmul", [BTs, UTs], [b1], out=b1[:, ss], lhsT=BTs[:, c, q_, :], rhs=UTs[:, qs],
                         start=False, stop=True)
                for q_ in range(2):
                    pt_ = hg * 2 + q_
                    ss = slice(256 + q_ * 128, 384 + q_ * 128)
                    S.op("dve", "tensor_tensor", [b1, Sst[pt_]], [Stmp], out=Stmp[:, 0:128], in0=b1[:, ss], in1=Sst[pt_][:],
                         op=ALU.add)
                    S.op("act", "activation", [Stmp, gam[q_]], [Sst[pt_]], out=Sst[pt_][:], in_=Stmp[:, 0:128], func=AF.Identity,
                         scale=gam[q_][:, c:c + 1])
                    S.op("dve", "tensor_scalar", [Stmp, gam[q_]], [Sbf[pt_]], out=Sbf[pt_][:], in0=Stmp[:, 0:128],
                         scalar1=gam[q_][:, c:c + 1], scalar2=None, op0=ALU.mult)
                if full:
                    for h in range(4):
                        os_ = slice(h * 64, (h + 1) * 64)
                        S.op("dve", "bn_stats", [b2], [gst], out=gst[:, h, :], in_=b2[:, os_])
                        S.op("dve", "bn_aggr", [gst], [gmv], out=gmv[:, h, :], in_=gst[:, h, :])
                    S.op("act", "activation", [gmv], [grs], out=grs[:], in_=gmv[:, :, 1], func=AF.Sqrt, bias=GN_EPS, scale=1.0)
                    S.op("dve", "reciprocal", [grs], [grs], out=grs[:], in_=grs[:])
                    for h in range(4):
                        os_ = slice(h * 64, (h + 1) * 64)
                        S.op("dve", "tensor_scalar", [b2, gmv, grs], [onb], out=onb[:, os_], in0=b2[:, os_],
                             scalar1=gmv[:, h, 0:1], scalar2=grs[:, h:h + 1], op0=ALU.subtract, op1=ALU.mult)
                    S.op("dve", "tensor_tensor", [onb, gnG], [onb], out=onb[:, 0:256], in0=onb[:, 0:256], in1=gnG[:], op=ALU.mult)
                    S.op("pool", "tensor_tensor", [onb, gnB], [onb], out=onb[:, 0:256], in0=onb[:, 0:256], in1=gnB[:], op=ALU.add)
                    for q_ in range(2):
                        S.op("pe", "matmul", [rkk[q_], hsel], [b0], out=b0[:, 256 + 2 * q_:258 + 2 * q_], lhsT=rkk[q_][:, Cc],
                             rhs=hsel[:], start=True, stop=True)
                    S.op("act", "activation", [b0], [bsum], out=bsum[:], in_=b0[:, 256:260], func=AF.Copy)
                    for h in range(4):
                        q_, pt_, hh = hd(h)
                        os_ = slice(h * 64, (h + 1) * 64)
                        S.op("dve", "scalar_tensor_tensor", [VTs, bsum, onb], [ofb], out=ofb[:, os_],
                             in0=VTs[:, c, q_, 64 * hh:64 * hh + 64], scalar=bsum[:, h:h + 1], in1=onb[:, os_],
                             op0=ALU.mult, op1=ALU.add)
                    for j in range(2):
                        S.op("pe", "matmul", [sg, gB_bf], [b2], out=b2[:, 256:512], lhsT=sg[:, j, Cc],
                             rhs=gB_bf[:, j, :], start=(j == 0), stop=(j == 1))
                    S.op("dve", "tensor_tensor", [ofb, b2], [ybb], out=ybb[:], in0=ofb[:, 0:256], in1=b2[:, 256:512], op=ALU.mult)
                    for q_ in range(2):
                        S.op("pe", "transpose", [ybb, ident_b], [pT[1]], out=pT[1][:, 4 + q_, :], in_=ybb[:, q_ * 128:(q_ + 1) * 128],
                             identity=ident_b[:])
                    for q_ in range(2):
                        S.op("act", "activation", [pT[1]], [yT], out=yT[:, fi, hg * 2 + q_, Cc], in_=pT[1][:, 4 + q_, :], func=AF.Copy)

            if stg != "st2":
                for cp in range(0, NCH, 2):
                    gens = [t_phase(cp, 0), t_phase(cp + 1, 1)]
                    alive = [True, True]
                    while any(alive):
                        for gi_ in range(2):
                            if alive[gi_]:
                                try:
                                    next(gens[gi_])
                                except StopIteration:
                                    alive[gi_] = False
                    state_phase(cp, 0)
                    state_phase(cp + 1, 1)

        for lg in range(4):
            if stg in ("st1", "st2", "st3"):
                break
            S.dma("pool", wa_bf[:], lwa_d[:, lg * 4:(lg + 1) * 4, :], [], [wa_bf])
            S.dma("pool", wx_bf[:], lwx_d[:, lg * 4:(lg + 1) * 4, :], [], [wx_bf])
            wu_ = load_w(wbf_in, 6592 + lg * 1024, 512)
            wgt = load_w(wbf_in, 6592 + lg * 1024 + 512, 512) if full else None
            for j in range(4):
                ct = lg * 4 + j
                uc, rg, ig, a1, a2, gx, hs, gl, t1, t2 = tmps[0:10]
                z = zs[0]
                zs_i[0] += 1
                p = proj_tile(wu_, j * 128, 128)
                S.op("pool", "tensor_copy", [ucarry], [z], out=z[:, 0:3], in_=ucarry[:, ct, :])
                S.op("act", "activation", [p], [z], out=z[:, 3:NT + 3], in_=p[:, :], func=AF.Copy)
                S.op("pool", "tensor_copy", [z], [ucarry], out=ucarry[:, ct, :], in_=z[:, NT:NT + 3])
                S.op("dve", "tensor_scalar", [z, cv], [uc], out=uc[:], in0=z[:, 3:NT + 3], scalar1=C("cw", 48 + ct),
                     scalar2=C("cb", ct), op0=ALU.mult, op1=ALU.add)
                for jj in range(3):
                    S.op("dve", "scalar_tensor_tensor", [z, cv, uc], [uc], out=uc[:], in0=z[:, jj:jj + NT],
                         scalar=C("cw", jj * 16 + ct), in1=uc[:], op0=ALU.mult, op1=ALU.add)
                S.op("act", "activation", [uc], [ucb], out=ucb[:], in_=uc[:], func=AF.Copy)
                p = next_pA()
                S.op("pe", "matmul", [wa_bf, ucb], [p], out=p[:, :], lhsT=wa_bf[:, j, :], rhs=ucb[:], start=True, stop=True)
                S.op("act", "activation", [p, cv], [rg], out=rg[:], in_=p[:, :], func=AF.Sigmoid, bias=C("ba", ct), scale=1.0)
                p = next_pA()
                S.op("pe", "matmul", [wx_bf, ucb], [p], out=p[:, :], lhsT=wx_bf[:, j, :], rhs=ucb[:], start=True, stop=True)
                S.op("act", "activation", [p, cv], [ig], out=ig[:], in_=p[:, :], func=AF.Sigmoid, bias=C("bx", ct), scale=1.0)
                S.op("act", "activation", [rg, cneg], [a1], out=a1[:], in_=rg[:], func=AF.Exp, scale=cneg[:, ct:ct + 1])
                S.op("act", "activation", [rg, cneg2], [a2], out=a2[:], in_=rg[:], func=AF.Exp, scale=cneg2[:, ct:ct + 1])
                S.op("dve", "tensor_scalar", [a2], [a2], out=a2[:], in0=a2[:], scalar1=1.0, scalar2=None, op0=ALU.min)
                S.op("act", "activation", [a2], [a2], out=a2[:], in_=a2[:], func=AF.Sqrt, bias=1.0, scale=-1.0)
                S.op("dve", "tensor_tensor", [ig, uc], [gx], out=gx[:], in0=ig[:], in1=uc[:], op=ALU.mult)
                S.op("dve", "scalar_tensor_tensor", [gx, bmask, a2], [gx], out=gx[:], in0=gx[:], scalar=bmask[:, b:b + 1],
                     in1=a2[:], op0=ALU.mult, op1=ALU.mult)
                S.op("dve", "tensor_tensor_scan", [a1, gx, hcarry], [hs], out=hs[:], data0=a1[:], data1=gx[:],
                     initial=hcarry[:, ct:ct + 1], op0=ALU.mult, op1=ALU.add)
                S.op("pool", "tensor_copy", [hs], [hcarry], out=hcarry[:, ct:ct + 1], in_=hs[:, NT - 1:NT])
                if full:
                    p = proj_tile(wgt, j * 128, 128)
                    S.op("act", "activation", [p], [gl], out=gl[:], in_=p[:, :], func=AF.Copy)
                    S.op("dve", "tensor_tensor", [gl], [t1], out=t1[:], in0=gl[:], in1=gl[:], op=ALU.mult)
                    S.op("dve", "tensor_scalar", [t1], [t1], out=t1[:], in0=t1[:], scalar1=0.044715, scalar2=1.0,
                         op0=ALU.mult, op1=ALU.add)
                    S.op("dve", "tensor_tensor", [t1, gl], [t1], out=t1[:], in0=t1[:], in1=gl[:], op=ALU.mult)
                    S.op("act", "activation", [t1], [t1], out=t1[:], in_=t1[:], func=AF.Sigmoid, scale=1.5957691216057308)
                    S.op("dve", "tensor_tensor", [t1, gl], [t1], out=t1[:], in0=t1[:], in1=gl[:], op=ALU.mult)
                    S.op("dve", "tensor_tensor", [t1, hs], [hs], out=hs[:], in0=t1[:], in1=hs[:], op=ALU.mult)
                    p = proj_small(10688 + ct * 128)
                    S.op("act", "activation", [p], [t1], out=t1[:], in_=p[:, :], func=AF.Sigmoid)
                    S.op("dve", "tensor_tensor", [t1, hs], [hs], out=hs[:], in0=t1[:], in1=hs[:], op=ALU.mult)
                    p = proj_small(12736 + ct * 128)
                    S.op("act", "activation", [p], [t2], out=t2[:], in_=p[:, :], func=AF.Sigmoid)
                    S.op("dve", "tensor_tensor", [t2, yT], [t2], out=t2[:], in0=t2[:], in1=yT[:, fi, ct, :], op=ALU.mult)
                    S.op("dve", "tensor_tensor", [t2, hs], [yT], out=yT[:, fi, ct, :], in0=t2[:], in1=hs[:], op=ALU.add)

        if dbg is not None and dbg[0] == "yT" and b == nsub - 1:
            for ct in range(16):
                S.op("dve", "tensor_copy", [yT], [tmps[0]], out=tmps[0][:], in_=yT[:, fi, ct, :])
                S.final.append(S.dma("sp", dbg_d[ct * 128:(ct + 1) * 128, :], tmps[0][:], [tmps[0]], []))
        if dbg is not None and dbg[0] == "S" and b == nsub - 1:
            for i in range(16):
                S.final.append(S.dma("sp", dbg_d[i * 128:(i + 1) * 128, 0:128], Sst[i][:], [Sst[i]], []))
            S.final.append(S.dma("sp", dbg_d[0:128, 128:144], hcarry[:], [hcarry], []))

    ms.close()
    S.barrier()
    cs = ExitStack()
    scope[0] = cs
    if dbg is None or dbg[0] == "full":
        res = sb("res", [128, NCH, D])
        hTc = sb("hTc", [128, 16, NT], BF16)
        qT = sb("qT", [128, 16, NT], BF16)
        KTm = sb("KTm", [128, 16, 256], BF16)
        Vm = sb("Vm", [128, 2, D], BF16)
        hidT = sb("hidT", [128, 11, NT], BF16)
        Gb = sb("Gb", [128, D])
        Bb = sb("Bb", [128, D])
        mx = sb("mx", [128, 1])
        ssum = sb("ssum", [128, 1])
        esb = sb("esb", [128, 256])
        pnb = sb("pnb", [128, 256], BF16)
        pTs = sb("pTs", [128, 2, NT], BF16)
        sgt = sb("sgt", [128, NT])

        def ln_tok(tt, gi, dstT):
            r_ = res[:, tt, :]
            ln_stats_ap(r_)
            S.op("dve", "tensor_scalar", [res, mv, rstd], [res], out=r_, in0=r_, scalar1=mv[:, 0:1],
                 scalar2=rstd[:, 0:1], op0=ALU.subtract, op1=ALU.mult)
            S.op("dve", "tensor_tensor", [res, Gb], [res], out=r_, in0=r_, in1=Gb[:], op=ALU.mult)
            S.op("pool", "tensor_tensor", [res, Bb], [res], out=r_, in0=r_, in1=Bb[:], op=ALU.add)
            if dstT is not None:
                S.op("act", "activation", [res], [xn], out=xn[:], in_=r_, func=AF.Copy)

                def ev(pt, k8, kt, tt=tt):
                    eng = "act" if k8 % 2 == 0 else "dve"
                    if eng == "act":
                        S.op("act", "activation", [pt], [dstT], out=dstT[:, kt, tt * 128:(tt + 1) * 128], in_=pt[:, k8, :], func=AF.Copy)
                    else:
                        S.op("dve", "tensor_copy", [pt], [dstT], out=dstT[:, kt, tt * 128:(tt + 1) * 128], in_=pt[:, k8, :])
                transpose16(xn, dstT, tt * 128, ev)

        def ln_stats_ap(ap):
            for c4 in range(4):
                S.op("dve", "bn_stats", [res], [st6], out=st6[:, c4, :], in_=ap[:, c4 * 512:(c4 + 1) * 512])
            S.op("dve", "bn_aggr", [st6], [mv], out=mv[:], in_=st6[:].rearrange("p a b -> p (a b)"))
            S.op("act", "activation", [mv], [rstd], out=rstd[:], in_=mv[:, 1:2], func=AF.Sqrt, bias=LN_EPS, scale=1.0)
            S.op("dve", "reciprocal", [rstd], [rstd], out=rstd[:], in_=rstd[:])

        def load_gb(gi):
            S.dma("sp", Gb[:], bc_d[gi], [], [Gb])
            S.dma("sp", Bb[:], bc_d[gi + 1], [], [Bb])

        def tokmajor_mm(lhsT_buf, lhs_ap_fn, wbuf, nk_total, evac):
            for nb in range(4):
                k0 = 0
                first = True
                while k0 < nk_total:
                    nk = min(16, nk_total - k0)
                    wtile = load_w(wbuf, nb * 512, 512, k0=k0, nk=nk)
                    for tt in range(NCH):
                        for kk_ in range(nk):
                            kt = k0 + kk_
                            S.op("pe", "matmul", [lhsT_buf, wtile], [pM[tt]], out=pM[tt][:, :], lhsT=lhs_ap_fn(kt, tt),
                                 rhs=wtile[:, kk_, 0:512], start=(kt == 0), stop=(kt == nk_total - 1))
                    k0 += nk
                for tt in range(NCH):
                    evac(tt, nb, pM[tt])

        for mt in range(2):
            S.dma("sp", xt[:], mem_d[mt * 128:(mt + 1) * 128, :], [], [xt])
            S.op("act", "activation", [xt], [xn], out=xn[:], in_=xt[:], func=AF.Copy)

            def evm(pt, k8, kt, mt=mt):
                S.op("dve", "tensor_copy", [pt], [qT], out=qT[:, kt, mt * 128:(mt + 1) * 128], in_=pt[:, k8, :])
            transpose16(xn, qT, 0, evm)
        for g4 in range(4):
            wtile = load_w(wbf_k, g4 * 512, 512)
            for j in range(4):
                dt_ = g4 * 4 + j
                p = next_pA()
                for kt in range(16):
                    S.op("pe", "matmul", [wtile, qT], [p], out=p[:, 0:256], lhsT=wtile[:, kt, j * 128:(j + 1) * 128],
                         rhs=qT[:, kt, 0:256], start=(kt == 0), stop=(kt == 15))
                S.op("act", "activation", [p], [KTm], out=KTm[:, dt_, :], in_=p[:, 0:256], func=AF.Copy)
        for nb in range(4):
            wtile = load_w(wbf_v, nb * 512, 512)
            for mt in range(2):
                p = next_pA()
                for kt in range(16):
                    S.op("pe", "matmul", [wtile, qT], [p], out=p[:, :], lhsT=qT[:, kt, mt * 128:(mt + 1) * 128],
                         rhs=wtile[:, kt, 0:512], start=(kt == 0), stop=(kt == 15))
                S.op("act", "activation", [p], [Vm], out=Vm[:, mt, nb * 512:(nb + 1) * 512], in_=p[:, :], func=AF.Copy)

        for fi in range(nfull):
            b = nsub - nfull + fi
            load_gb(2)
            for tt in range(NCH):
                r0 = b * NT + tt * 128
                S.dma("sp", xt[:], x_d[r0:r0 + 128, :], [], [xt])
                ln_stats(xt)
                r_ = res[:, tt, :]
                S.op("dve", "tensor_scalar", [xt, mv, rstd], [res], out=r_, in0=xt[:], scalar1=mv[:, 0:1],
                     scalar2=rstd[:, 0:1], op0=ALU.subtract, op1=ALU.mult)
                S.op("dve", "tensor_tensor", [res, Gb], [res], out=r_, in0=r_, in1=Gb[:], op=ALU.mult)
                S.op("pool", "tensor_tensor", [res, Bb], [res], out=r_, in0=r_, in1=Bb[:], op=ALU.add)

            def ev_res(tt, nb, p):
                cs_ = slice(nb * 512, (nb + 1) * 512)
                S.op("dve", "scalar_tensor_tensor", [res, p], [res], out=res[:, tt, cs_], in0=res[:, tt, cs_], scalar=ALPHA,
                     in1=p[:, :], op0=ALU.mult, op1=ALU.add)
            tokmajor_mm(yT, lambda kt, tt, fi=fi: yT[:, fi, kt, tt * 128:(tt + 1) * 128], wbf_out, 16, ev_res)
            load_gb(4)
            for tt in range(NCH):
                ln_tok(tt, 4, hTc)
            for g4 in range(4):
                wtile = load_w(wbf_q, g4 * 512, 512)
                for j in range(4):
                    dt_ = g4 * 4 + j
                    p = next_pA()
                    for kt in range(16):
                        S.op("pe", "matmul", [wtile, hTc], [p], out=p[:, :], lhsT=wtile[:, kt, j * 128:(j + 1) * 128],
                             rhs=hTc[:, kt, :], start=(kt == 0), stop=(kt == 15))
                    S.op("act", "activation", [p], [qT], out=qT[:, dt_, :], in_=p[:, :], func=AF.Copy, scale=512 ** -0.5)
            for hd_ in range(4):
                for tt in range(NCH):
                    p = next_pA()
                    for d4 in range(4):
                        dt_ = hd_ * 4 + d4
                        S.op("pe", "matmul", [qT, KTm], [p], out=p[:, 0:256], lhsT=qT[:, dt_, tt * 128:(tt + 1) * 128],
                             rhs=KTm[:, dt_, :], start=(d4 == 0), stop=(d4 == 3))
                    S.op("dve", "tensor_reduce", [p], [mx], out=mx[:], in_=p[:, 0:256], axis=mybir.AxisListType.X, op=ALU.max)
                    S.op("dve", "tensor_scalar", [mx], [mx], out=mx[:], in0=mx[:], scalar1=-1.0, scalar2=None, op0=ALU.mult)
                    S.op("act", "activation", [p, mx], [esb, ssum], out=esb[:], in_=p[:, 0:256], func=AF.Exp, bias=mx[:, 0:1],
                         scale=1.0, accum_out=ssum[:])
                    S.op("dve", "reciprocal", [ssum], [ssum], out=ssum[:], in_=ssum[:])
                    S.op("dve", "tensor_scalar", [esb, ssum], [pnb], out=pnb[:], in0=esb[:], scalar1=ssum[:, 0:1], scalar2=None,
                         op0=ALU.mult)
                    for mt in range(2):
                        S.op("pe", "transpose", [pnb, ident_b], [pT[0]], out=pT[0][:, mt, :], in_=pnb[:, mt * 128:(mt + 1) * 128],
                             identity=ident_b[:])
                    S.op("act", "activation", [pT[0]], [pTs], out=pTs[:, :, tt * 128:(tt + 1) * 128], in_=pT[0][:, 0:2, :], func=AF.Copy)
                for d4 in range(4):
                    dt_ = hd_ * 4 + d4
                    p = next_pA()
                    for mt in range(2):
                        S.op("pe", "matmul", [Vm, pTs], [p], out=p[:, :], lhsT=Vm[:, mt, dt_ * 128:(dt_ + 1) * 128],
                             rhs=pTs[:, mt, :], start=(mt == 0), stop=(mt == 1))
                    S.op("act", "activation", [p], [hTc], out=hTc[:, dt_, :], in_=p[:, :], func=AF.Copy)
            tokmajor_mm(hTc, lambda kt, tt: hTc[:, kt, tt * 128:(tt + 1) * 128], wbf_o, 16, ev_res)
            load_gb(6)
            for tt in range(NCH):
                ln_tok(tt, 6, hTc)
            for hh_ in range(4):
                for g4 in range(3):
                    ntile = 4 if g4 < 2 else 3
                    c0 = hh_ * 1408 + g4 * 512
                    wgt_ = load_w(wbf_g, c0, ntile * 128)
                    wut_ = load_w(wbf_u, c0, ntile * 128)
                    for j in range(ntile):
                        ht_ = g4 * 4 + j
                        pg = next_pA()
                        for kt in range(16):
                            S.op("pe", "matmul", [wgt_, hTc], [pg], out=pg[:, :], lhsT=wgt_[:, kt, j * 128:(j + 1) * 128],
                                 rhs=hTc[:, kt, :], start=(kt == 0), stop=(kt == 15))
                        S.op("act", "activation", [pg], [sgt], out=sgt[:], in_=pg[:, :], func=AF.Silu)
                        pu = next_pA()
                        for kt in range(16):
                            S.op("pe", "matmul", [wut_, hTc], [pu], out=pu[:, :], lhsT=wut_[:, kt, j * 128:(j + 1) * 128],
                                 rhs=hTc[:, kt, :], start=(kt == 0), stop=(kt == 15))
                        S.op("dve", "tensor_tensor", [sgt, pu], [hidT], out=hidT[:, ht_, :], in0=sgt[:], in1=pu[:, :], op=ALU.mult)

                def ev_ffn(tt, nb, p, hh_=hh_):
                    cs_ = slice(nb * 512, (nb + 1) * 512)
                    if hh_ == 0:
                        S.op("dve", "scalar_tensor_tensor", [res, p], [res], out=res[:, tt, cs_], in0=res[:, tt, cs_], scalar=ALPHA,
                             in1=p[:, :], op0=ALU.mult, op1=ALU.add)
                    else:
                        S.op("dve", "tensor_tensor", [res, p], [res], out=res[:, tt, cs_], in0=res[:, tt, cs_], in1=p[:, :], op=ALU.add)
                for nb in range(4):
                    for (k0, nk) in [(0, 11)]:
                        wtile = load_w(wbf_d, nb * 512, 512, k0=hh_ * 11 + k0, nk=nk)
                        for tt in range(NCH):
                            for kk_ in range(nk):
                                kt = k0 + kk_
                                S.op("pe", "matmul", [hidT, wtile], [pM[tt]], out=pM[tt][:, :], lhsT=hidT[:, kt, tt * 128:(tt + 1) * 128],
                                     rhs=wtile[:, kk_, 0:512], start=(kt == 0), stop=(kt == 10))
                    for tt in range(NCH):
                        ev_ffn(tt, nb, pM[tt])
            load_gb(8)
            for tt in range(NCH):
                ln_tok(tt, 8, None)
                S.final.append(S.dma("sp", y_d[fi * NT + tt * 128:fi * NT + (tt + 1) * 128, :], res[:, tt, :], [res], []))
    else:
        zt = sb("zt", [128, D])
        S.op("dve", "memset", [], [zt], zt[:], 0.0)
        for i in range(nfull * NCH):
            S.final.append(S.dma("sp", y_d[i * 128:(i + 1) * 128, :], zt[:], [zt], []))

    with nc.Block() as block:
        S.emit(block)
    cs.close()
    es.close()
    return nc


def _perm_cols():
    u0, gl0, rw0 = 0, 2048, 4096
    r0, k0, v0 = rw0, rw0 + 2048, rw0 + 4096
    lo0 = rw0 + 6144
    ga0 = rw0 + 6592
    gb0 = ga0 + 2048
    idx = []
    for hg in range(8):
        idx += list(range(k0 + hg * 256, k0 + hg * 256 + 256))
        idx += list(range(v0 + hg * 256, v0 + hg * 256 + 256))
        idx += list(range(r0 + hg * 256, r0 + hg * 256 + 256))
    idx += list(range(lo0, lo0 + 448))
    for lg in range(4):
        idx += list(range(u0 + lg * 512, u0 + lg * 512 + 512))
        idx += list(range(gl0 + lg * 512, gl0 + lg * 512 + 512))
    idx += list(range(ga0, ga0 + 2048))
    idx += list(range(gb0, gb0 + 2048))
    return np.asarray(idx)


def host_prep(inp, nsub=16, nfull=3, ends=None):
    if ends is None:
        ends = [nsub] * 8
    f = np.float32
    g = {k: np.asarray(v, dtype=f) for k, v in inp.items()}
    perm = _perm_cols()
    w_in = np.ascontiguousarray(g["w_in"][0][:, perm])
    mu_full = np.zeros(NIN, f)
    mu_full[4096:4096 + 6592] = g["rw_mu"][0]
    mu_p = mu_full[perm]
    cvec = np.zeros((128, CM.n), f)

    def put(name, arr2d):
        o = CM.off[name]
        cvec[:, o:o + arr2d.shape[0]] = arr2d.T
    put("lng", g["ln_in_g"].reshape(16, 128))
    put("lnb", g["ln_in_b"].reshape(16, 128))
    put("mu_rw", mu_p[0:6144].reshape(48, 128))
    lo = mu_p[6144:6144 + 448]
    mlo = np.zeros((4, 128), f)
    mlo[0, :96] = lo[0:96]
    mlo[1, :96] = lo[96:192]
    mlo[2] = lo[192:320]
    mlo[3] = lo[320:448]
    put("mu_lo", mlo)
    put("w0", g["rw_w0"][0].reshape(16, 128))
    put("a0", g["rw_a0"][0].reshape(16, 128))
    put("kk", g["rw_kk"][0].reshape(16, 128))
    put("ka", g["rw_ka"][0].reshape(16, 128))
    put("rk", g["rw_rk"][0].reshape(16, 128))
    cw = g["conv_w"][0]
    put("cw", np.stack([cw[j].reshape(16, 128) for j in range(4)], 0).reshape(64, 128))
    put("cb", g["conv_b"][0].reshape(16, 128))
    put("ba", g["lru_ba"][0].reshape(16, 128))
    put("bx", g["lru_bx"][0].reshape(16, 128))
    put("lam", g["lru_lambda"][0].reshape(16, 128))
    bc = np.stack([np.broadcast_to(v.reshape(1, D), (128, D)) for v in
                   [g["rw_gn_g"][0], g["rw_gn_b"][0], g["ln_in_g"], g["ln_in_b"], g["ln1_g"][0], g["ln1_b"][0],
                    g["ln2_g"][0], g["ln2_b"][0], g["ln3_g"][0], g["ln3_b"][0]]], 0).astype(f)
    shared = {
        "w_in": w_in, "w_out": g["w_out"][0], "xa_wq": g["xa_wq"][0], "xa_wk": g["xa_wk"][0], "xa_wv": g["xa_wv"][0],
        "xa_wo": g["xa_wo"][0], "ffn_wg": g["ffn_wg"][0], "ffn_wu": g["ffn_wu"][0], "ffn_wd": g["ffn_wd"][0],
        "mem": g["mem"][0], "cvec": cvec,
        "lru_wa": np.ascontiguousarray(g["lru_wa"][0].transpose(1, 0, 2)),
        "lru_wx": np.ascontiguousarray(g["lru_wx"][0].transpose(1, 0, 2)),
        "rw_wB": g["rw_wB"][0], "rw_aB": g["rw_aB"][0], "rw_gB": g["rw_gB"][0], "bcast": np.ascontiguousarray(bc),
    }
    x = g["x"][0]
    T = nsub * BLK
    maps = []
    for e in ends:
        n_valid = e
        xl = np.zeros((T, D), f)
        xl[T - n_valid * BLK:] = x[0:e * BLK]
        bmk = np.zeros((128, 16), f)
        bmk[:, nsub - n_valid:nsub] = 1.0
        m = dict(shared)
        m["x"] = xl
        m["bmask"] = bmk
        maps.append(m)
    return maps


ENDS = [3, 6, 9, 12, 14, 16]
NFULL = 3


_NC_CACHE = {}


def kernel(**inputs):
    maps = host_prep(inputs, 16, NFULL, ENDS)
    if "nc" not in _NC_CACHE:
        _NC_CACHE["nc"] = build(16, NFULL)
    n = len(ENDS)
    res = run_bass_kernel_spmd(_NC_CACHE["nc"], maps, core_ids=list(range(n)))
    out = np.zeros((16 * BLK, D), np.float32)
    for gsb in range(16):
        for i, e in enumerate(ENDS):
            if e - NFULL <= gsb < e:
                fi = gsb - (e - NFULL)
                y = np.asarray(res.results[i]["y"], dtype=np.float32)
                out[gsb * BLK:(gsb + 1) * BLK] = y[fi * BLK:(fi + 1) * BLK]
                break
    return out.reshape(1, 8192, D)
```
